# Optimizing a Trainium2 kernel written in Bass

```python
import math
import jax, jax.numpy as jnp
from jax import lax
import numpy as np

D_MODEL = 2048
BATCH = 16
SEQ = 2048
DEPTH = 4

CHUNK = 64
N_A_LAYERS = DEPTH // 2
N_B_LAYERS = DEPTH - N_A_LAYERS
SSM_WIDTH = D_MODEL
SSM_GROUP = 16
SSM_GROUPS = SSM_WIDTH // SSM_GROUP
SSM_STATE = 64
HEAD_DIM = 128
N_HEADS = D_MODEL // HEAD_DIM
N_KV_HEADS = 4
Q_PER_KV = N_HEADS // N_KV_HEADS
Q_BLOCK = 128
D_FF = 4 * D_MODEL
PLE_DIM = 256
NORM_EPS = 1e-6
DT_MIN = 1e-3
DT_MAX = 1e-1

kernel_name = "yoco_s5_stickbreaking_hybrid"


def rms_norm(x, g):
    x32 = x.astype(jnp.float32)
    y = x32 * lax.rsqrt(jnp.mean(x32 * x32, axis=-1, keepdims=True) + NORM_EPS)
    return (y * g.astype(jnp.float32)).astype(x.dtype)


def _complex_affine_combine(e1, e2):
    a1r, a1i, b1r, b1i = e1
    a2r, a2i, b2r, b2i = e2
    ar = a2r * a1r - a2i * a1i
    ai = a2r * a1i + a2i * a1r
    br = a2r * b1r - a2i * b1i + b2r
    bi = a2r * b1i + a2i * b1r + b2i
    return (ar, ai, br, bi)


def s5_mixer(h, w_in, a_re, a_im, log_step, b_re, b_im, c_re, c_im, d_skip, w_glu):
    f32 = jnp.float32
    bsz, seq, _ = h.shape
    u = (h @ w_in).astype(f32).reshape(bsz, seq, SSM_GROUPS, SSM_GROUP)
    dt = jnp.exp(log_step.astype(f32))[:, None]
    ar = a_re.astype(f32)
    ai = a_im.astype(f32)
    mag = jnp.exp(ar * dt)
    lbr = mag * jnp.cos(ai * dt)
    lbi = mag * jnp.sin(ai * dt)
    den = ar * ar + ai * ai
    nr = lbr - 1.0
    fr = (nr * ar + lbi * ai) / den
    fi = (lbi * ar - nr * ai) / den
    br = b_re.astype(f32)
    bi = b_im.astype(f32)
    bbr = fr[..., None] * br - fi[..., None] * bi
    bbi = fr[..., None] * bi + fi[..., None] * br
    bu_re = jnp.einsum('bsgh,gph->bsgp', u, bbr)
    bu_im = jnp.einsum('bsgh,gph->bsgp', u, bbi)
    lam_re = jnp.broadcast_to(lbr[None, None], bu_re.shape)
    lam_im = jnp.broadcast_to(lbi[None, None], bu_im.shape)
    _, _, xs_re, xs_im = lax.associative_scan(
        _complex_affine_combine, (lam_re, lam_im, bu_re, bu_im), axis=1)
    y = (jnp.einsum('bsgp,ghp->bsgh', xs_re, c_re.astype(f32))
         - jnp.einsum('bsgp,ghp->bsgh', xs_im, c_im.astype(f32))
         + d_skip.astype(f32).reshape(SSM_GROUPS, SSM_GROUP) * u)
    y = jax.nn.gelu(y.reshape(bsz, seq, SSM_WIDTH)).astype(h.dtype)
    val, gate = jnp.split(y @ w_glu, 2, axis=-1)
    return val * jax.nn.sigmoid(gate)


def stick_breaking_attention(q, k, v):
    seq = q.shape[1]
    scale = HEAD_DIM ** -0.5
    outs = []
    for blk in range(seq // Q_BLOCK):
        q0 = blk * Q_BLOCK
        end = q0 + Q_BLOCK
        qb = q[:, q0:end]
        kb = k[:, :end]
        vb = v[:, :end]
        z = jnp.einsum('bqhgd,bkhd->bhgqk', qb, kb,
                       preferred_element_type=jnp.float32) * scale
        t_pos = q0 + jnp.arange(Q_BLOCK)[:, None]
        s_pos = jnp.arange(end)[None, :]
        strict = s_pos < t_pos
        log_keep = jnp.where(strict, jax.nn.log_sigmoid(-z), 0.0)
        after = lax.cumsum(log_keep, axis=log_keep.ndim - 1, reverse=True) - log_keep
        w = jnp.where(strict, jnp.exp(jax.nn.log_sigmoid(z) + after), 0.0)
        outs.append(jnp.einsum('bhgqk,bkhd->bqhgd', w.astype(vb.dtype), vb))
    return jnp.concatenate(outs, axis=1)


def setup_inputs(seed: int = 0) -> dict:
    key = jax.random.key(seed)
    ks = jax.random.split(key, 32)
    f32 = jnp.float32
    D, H, G, P, Hg = D_MODEL, SSM_WIDTH, SSM_GROUPS, SSM_STATE, SSM_GROUP
    nA, nB = N_A_LAYERS, N_B_LAYERS
    kvw = N_KV_HEADS * HEAD_DIM
    qw = N_HEADS * HEAD_DIM

    def nrm(k, shape, scale):
        return jax.random.normal(k, shape, f32) * scale

    def gain(k, shape):
        return 1.0 + 0.01 * jax.random.normal(k, shape, f32)

    n_idx = jnp.arange(P, dtype=f32)
    return {
        "x": nrm(ks[0], (BATCH, SEQ, D), 1.0),
        "p": nrm(ks[1], (DEPTH, BATCH, SEQ, PLE_DIM), 1.0),
        "a_norm_pre": gain(ks[2], (nA, D)),
        "a_norm_post": gain(ks[3], (nA, D)),
        "ssm_w_in": nrm(ks[4], (nA, D, H), D ** -0.5),
        "ssm_a_re": -0.5 + 0.01 * jax.random.normal(ks[5], (nA, G, P), f32),
        "ssm_a_im": math.pi * n_idx + 0.01 * jax.random.normal(ks[6], (nA, G, P), f32),
        "ssm_log_step": jax.random.uniform(ks[7], (nA, G), f32,
                                           minval=math.log(DT_MIN), maxval=math.log(DT_MAX)),
        "ssm_b_re": nrm(ks[8], (nA, G, P, Hg), (2 * Hg) ** -0.5),
        "ssm_b_im": nrm(ks[9], (nA, G, P, Hg), (2 * Hg) ** -0.5),
        "ssm_c_re": nrm(ks[10], (nA, G, Hg, P), (2 * P) ** -0.5),
        "ssm_c_im": nrm(ks[11], (nA, G, Hg, P), (2 * P) ** -0.5),
        "ssm_d": nrm(ks[12], (nA, H), 0.5),
        "ssm_w_glu": nrm(ks[13], (nA, H, 2 * D), H ** -0.5),
        "kv_norm": gain(ks[14], (D,)),
        "w_k": nrm(ks[15], (D, kvw), D ** -0.5),
        "w_v": nrm(ks[16], (D, kvw), D ** -0.5),
        "b_norm_pre": gain(ks[17], (nB, D)),
        "b_norm_post": gain(ks[18], (nB, D)),
        "w_q": nrm(ks[19], (nB, D, qw), D ** -0.5),
        "w_o": nrm(ks[20], (nB, qw, D), qw ** -0.5),
        "mlp_norm_pre": gain(ks[21], (DEPTH, D)),
        "mlp_norm_post": gain(ks[22], (DEPTH, D)),
        "mlp_w1": nrm(ks[23], (DEPTH, D, D_FF), D ** -0.5),
        "mlp_w2": nrm(ks[24], (DEPTH, D_FF, D), D_FF ** -0.5),
        "ple_w": nrm(ks[25], (DEPTH, PLE_DIM, D), PLE_DIM ** -0.5),
        "ple_gate": nrm(ks[26], (DEPTH, D, D), D ** -0.5),
    }


def reference(x, p, a_norm_pre, a_norm_post, ssm_w_in, ssm_a_re, ssm_a_im, ssm_log_step,
              ssm_b_re, ssm_b_im, ssm_c_re, ssm_c_im, ssm_d, ssm_w_glu,
              kv_norm, w_k, w_v, b_norm_pre, b_norm_post, w_q, w_o,
              mlp_norm_pre, mlp_norm_post, mlp_w1, mlp_w2, ple_w, ple_gate):
    bsz, seq, _ = x.shape
    h = x
    k_sh = None
    v_sh = None
    for i in range(DEPTH):
        if i < N_A_LAYERS:
            j = i
            m = s5_mixer(rms_norm(h, a_norm_pre[j]), ssm_w_in[j], ssm_a_re[j], ssm_a_im[j],
                         ssm_log_step[j], ssm_b_re[j], ssm_b_im[j], ssm_c_re[j], ssm_c_im[j],
                         ssm_d[j], ssm_w_glu[j])
            h = h + rms_norm(m, a_norm_post[j])
        else:
            j = i - N_A_LAYERS
            if j == 0:
                kv_in = rms_norm(h, kv_norm)
                k_sh = (kv_in @ w_k).reshape(bsz, seq, N_KV_HEADS, HEAD_DIM)
                v_sh = (kv_in @ w_v).reshape(bsz, seq, N_KV_HEADS, HEAD_DIM)
            q = (rms_norm(h, b_norm_pre[j]) @ w_q[j]).reshape(
                bsz, seq, N_KV_HEADS, Q_PER_KV, HEAD_DIM)
            o = stick_breaking_attention(q, k_sh, v_sh).reshape(bsz, seq, N_HEADS * HEAD_DIM)
            h = h + rms_norm(o @ w_o[j], b_norm_post[j])
        f = rms_norm(h, mlp_norm_pre[i]) @ mlp_w1[i]
        f = jnp.square(jax.nn.relu(f)) @ mlp_w2[i]
        h = h + rms_norm(f, mlp_norm_post[i])
        h = h + (p[i] @ ple_w[i]) * jax.nn.sigmoid(h @ ple_gate[i])
    return h
```

```python
import contextlib
import math
import numpy as np
import concourse.bass as bass
import concourse.mybir as mybir
from concourse.bass_utils import run_bass_kernel_spmd

F32 = mybir.dt.float32
BF16 = mybir.dt.bfloat16
ALU = mybir.AluOpType
AF = mybir.ActivationFunctionType

D = 2048
DFF = 8192
DEPTH = 4
TT = 512
T8 = 8
NCH = TT // T8
EPS = 1e-6
SCALE = 128 ** -0.5


class Op:
    __slots__ = ("eng", "fn", "reads", "writes", "dma", "deps", "signal", "semval")

    def __init__(self, eng, fn, reads, writes, dma):
        self.eng = eng
        self.fn = fn
        self.reads = reads
        self.writes = writes
        self.dma = dma
        self.deps = []
        self.signal = False
        self.semval = None


class Prog:
    SAME_ENGINE_SYNC = ("dve", "pool")
    NDMA = {"sp": 24, "pool": 12, "act": 4}
    ROT = 30000

    def __init__(self, nc):
        self.nc = nc
        self.ops = []
        self.final_dma = []

    def engine(self, name):
        nc = self.nc
        return {"pe": nc.tensor, "act": nc.scalar, "dve": nc.vector, "pool": nc.gpsimd, "sp": nc.sync}[name]

    def op(self, eng, fn, reads=(), writes=()):
        o = Op(eng, fn, tuple(reads), tuple(writes), False)
        self.ops.append(o)
        return o

    def dma(self, eng, fn, reads=(), writes=(), final=False):
        o = Op(eng, fn, tuple(reads), tuple(writes), True)
        self.ops.append(o)
        if final:
            self.final_dma.append(o)
        return o

    def alias(self, new, old):
        self.ops.append(("alias", tuple(new), tuple(old)))

    def emit(self, stack):
        nc = self.nc
        ops = self.ops
        last_write = {}
        reads_since = {}
        engs = ("pe", "act", "dve", "pool", "sp")
        seen = {e: {} for e in engs}
        seen_dma = {e: set() for e in engs}
        for i, o in enumerate(ops):
            if isinstance(o, tuple):
                _, new, old = o
                pend = set()
                for r in old:
                    if r in last_write:
                        pend.add(last_write[r])
                    pend.update(reads_since.get(r, ()))
                for r in new:
                    last_write.pop(r, None)
                    reads_since[r] = sorted(pend)
                continue
            deps = set()
            for r in o.reads:
                lw = last_write.get(r)
                if lw is not None:
                    deps.add(lw)
            for w in o.writes:
                lw = last_write.get(w)
                if lw is not None:
                    deps.add(lw)
                deps.update(reads_since.get(w, ()))
            deps.discard(i)
            best = {}
            final = []
            for d in deps:
                od = ops[d]
                if od.dma:
                    if d in seen_dma[o.eng]:
                        continue
                    seen_dma[o.eng].add(d)
                    final.append(d)
                    od.signal = True
                else:
                    if od.eng == o.eng and not o.dma:
                        if od.eng not in self.SAME_ENGINE_SYNC:
                            continue
                    if seen[o.eng].get(od.eng, -1) >= d:
                        continue
                    if best.get(od.eng, -1) < d:
                        best[od.eng] = d
            for e, d in best.items():
                seen[o.eng][e] = d
                ops[d].signal = True
                final.append(d)
            o.deps = final
            for r in o.reads:
                reads_since.setdefault(r, []).append(i)
            for w in o.writes:
                last_write[w] = i
                reads_since[w] = []
        for o in self.final_dma:
            o.signal = True
        nsig = {e: 0 for e in engs}
        for o in ops:
            if not isinstance(o, tuple) and o.signal and not o.dma:
                nsig[o.eng] += 1
        csems = {}
        for e in engs:
            n = nsig[e] // self.ROT + 1
            csems[e] = [stack.enter_context(nc.semaphore(f"c_{e}_{k}")) for k in range(n)]
        dsems = {e: [stack.enter_context(nc.semaphore(f"d_{e}_{k}")) for k in range(n)]
                 for e, n in self.NDMA.items()}
        ccount = {e: 0 for e in engs}
        dcount = {e: 0 for e in self.NDMA}
        for o in ops:
            if isinstance(o, tuple):
                continue
            eng = self.engine(o.eng)
            for d in o.deps:
                sem, val = ops[d].semval
                eng.wait_ge(sem, val)
            if o.dma:
                n = dcount[o.eng]
                dcount[o.eng] += 1
                pool = dsems[o.eng]
                slot = n % len(pool)
                cnt = n // len(pool) + 1
                if cnt > 1:
                    eng.wait_ge(pool[slot], 16 * (cnt - 1))
                ins = o.fn(eng)
                ins.then_inc(pool[slot], 16)
                o.semval = (pool[slot], 16 * cnt)
            else:
                ins = o.fn(eng)
                if o.signal:
                    c = ccount[o.eng]
                    ccount[o.eng] += 1
                    sem = csems[o.eng][c // self.ROT]
                    ins.then_inc(sem, 1)
                    o.semval = (sem, c % self.ROT + 1)
        sp = nc.sync
        for o in self.final_dma:
            sem, val = o.semval
            sp.wait_ge(sem, val)
        self.stats = dict(n_ops=len(ops), nsig=nsig, ndma=dcount)


def full_plan():
    plan = []
    for i in range(DEPTH):
        plan += [("mix", i), ("mlp", i), ("ple", i)]
    return plan


def build(S, NSEQ, plan):
    NT = S // TT
    nc = bass.Bass("TRN2", target_bir_lowering=False)
    layers_a = sorted({i for (k, i) in plan if k == "mix" and i < 2})

    def din(name, shape):
        return nc.dram_tensor(name, list(shape), F32, kind="ExternalInput").ap()

    x_d = din("x", [NSEQ, S, D])
    pT_d = din("pT", [DEPTH, NSEQ, 256, S])
    w_in_d = din("ssm_w_in", [2, D, D])
    w_glu_d = din("ssm_w_glu", [2, D, 2 * D])
    w_k_d = din("w_k", [D, 512])
    w_v_d = din("w_v", [D, 512])
    w_q_d = din("w_q", [2, D, D])
    w_o_d = din("w_o", [2, D, D])
    w1_d = din("mlp_w1", [DEPTH, D, DFF])
    w2_d = din("mlp_w2", [DEPTH, DFF, D])
    plew_d = din("ple_w", [DEPTH, 256, D])
    gate_d = din("ple_gate", [DEPTH, D, D])
    gcols_d = din("gcols", [128, 9 * 16])
    grows_d = din("grows", [8, D])
    aL1_d = din("aL1", [2, 3, 16, 512])
    bL1_d = din("bL1", [2, 2, 16, 128, 512])
    a3_d = din("a3", [2, 3, 128, 64])
    bL2_d = din("bL2", [2, 2, 16, 128, 512])
    cL2_d = din("cL2", [2, 2, 16, 128, 512])
    dmat_d = din("dmat", [2, 16, 128, 128])
    cst_d = din("consts", [128, 1024])
    out_d = nc.dram_tensor("out", [NSEQ, S, D], F32, kind="ExternalOutput").ap()
    Vd = nc.dram_tensor("Vd", [2, 16, 128, 8192], BF16, kind="Internal").ap()
    Od = nc.dram_tensor("Od", [2, 16, 128, 8192], BF16, kind="Internal").ap()
    BDd = nc.dram_tensor("BDd", [2, 16, 128, 1024], BF16, kind="Internal").ap()
    Gsd = nc.dram_tensor("Gsd", [2, 1024, 128], F32, kind="Internal").ap()

    st = contextlib.ExitStack()
    with st:
        def sb(name, shape, dt):
            return st.enter_context(nc.sbuf_tensor("s_" + name, list(shape), dt))

        h = sb("h", [128, 4, D], F32)
        xnT = sb("xnT", [128, 16, TT], BF16)
        pans = [sb(f"pan{i}", [128, 8, 512], BF16) for i in range(3)]
        mb = sb("mb", [128, 4, D], BF16)
        gbc = sb("gbc", [128, D], F32)
        kTc = sb("kTc", [128, 4, S], BF16)
        vvc = sb("vvc", [128, S // 128, 512], BF16)
        sg = [sb(f"sg{i}", [128, 512], F32) for i in range(2)]
        gcols = sb("gcols", [128, 9 * 16], F32)
        cst = sb("cst", [128, 1024], BF16)
        ident_f = sb("ident_f", [128, 128], F32)
        stat = sb("stat", [128, 64], F32)
        Xst = sb("Xst", [128, 2, 2, 64], F32)
        A8 = sb("A8", [128, 2, 2, 64], F32)
        R = sb("R", [128, 32768], BF16)
        banks = [st.enter_context(nc.psum_tensor(f"ps{i}", [128, 512], F32)) for i in range(8)]

        ident = cst[:, 0:128]
        Umat = cst[:, 128:256]
        Lomat = cst[:, 256:384]
        maskd = cst[:, 384:896]

        P = Prog(nc)
        state = {"bank": 0, "pan": 0, "sg": 0, "att": 0, "zb": 0, "recording": True, "plog": [], "issued": 0}

        def nb():
            b = state["bank"]
            state["bank"] = (b + 1) % 8
            return b

        def bres(b):
            return ("ps", b)

        def carve(off, shape, dt):
            n = int(np.prod(shape[1:]))
            if dt == F32:
                v = R[:, off // 2: off // 2 + 2 * n].bitcast(F32)
            else:
                v = R[:, off // 2: off // 2 + n]
            if len(shape) == 3:
                v = v.rearrange("p (a b) -> p a b", a=shape[1])
            elif len(shape) == 4:
                v = v.rearrange("p (a b c) -> p a b c", a=shape[1], b=shape[2])
            elif len(shape) == 5:
                v = v.rearrange("p (a b c d) -> p a b c d", a=shape[1], b=shape[2], c=shape[3])
            return v

        def panel(src, nkc=8):
            n = state["pan"]
            state["pan"] = n + 1
            if state["recording"]:
                state["plog"].append((src, nkc))
                return pans[n % 3], ("pan", n % 3)
            plog = state["plog"]
            while state["issued"] < min(n + 3, len(plog)):
                i = state["issued"]
                state["issued"] += 1
                psrc, pk = plog[i]
                pt = pans[i % 3]
                P.dma("pool", lambda e, pt=pt, psrc=psrc, pk=pk: e.dma_start(
                    out=pt[:, 0:pk, :], in_=psrc.rearrange("(kc p) n -> p kc n", p=128)),
                    writes=[("pan", i % 3)])
            return pans[n % 3], ("pan", n % 3)

        def mm(out, lhsT, rhs, start, stop, reads, writes):
            P.op("pe", lambda e, out=out, lhsT=lhsT, rhs=rhs, start=start, stop=stop: e.matmul(
                out, lhsT, rhs, start=start, stop=stop), reads=reads, writes=writes)

        def load_const():
            P.dma("pool", lambda e: e.dma_start(out=cst[:, :], in_=cst_d),
                  writes=["cst"])
            P.dma("sp", lambda e: e.dma_start(out=ident_f[:, :], in_=cst_d[:, 0:128]), writes=["ident_f"])
            P.dma("sp", lambda e: e.dma_start(out=gcols[:, :], in_=gcols_d), writes=["gcols"])

        def norm_T(gi, do_norm=True):
            for tt in range(4):
                if do_norm:
                    P.op("act", lambda e, tt=tt: e.activation(
                        out=mb[:, tt, :], in_=h[:, tt, :], func=AF.Square, accum_out=stat[:, tt:tt + 1]),
                        reads=["h"], writes=[("mb", tt), ("stat", tt)])
                    P.op("dve", lambda e, tt=tt: e.tensor_scalar(
                        out=stat[:, 8 + tt:9 + tt], in0=stat[:, tt:tt + 1], scalar1=1.0 / D, scalar2=EPS,
                        op0=ALU.mult, op1=ALU.add), reads=[("stat", tt)], writes=[("rstd", tt)])
                    P.op("act", lambda e, tt=tt: e.activation(
                        out=stat[:, 8 + tt:9 + tt], in_=stat[:, 8 + tt:9 + tt], func=AF.Sqrt),
                        reads=[("rstd", tt)], writes=[("rstd", tt)])
                    P.op("dve", lambda e, tt=tt: e.reciprocal(
                        out=stat[:, 8 + tt:9 + tt], in_=stat[:, 8 + tt:9 + tt]),
                        reads=[("rstd", tt)], writes=[("rstd", tt)])
                    P.op("dve", lambda e, tt=tt: e.tensor_scalar(
                        out=mb[:, tt, :], in0=h[:, tt, :], scalar1=stat[:, 8 + tt:9 + tt], scalar2=None,
                        op0=ALU.mult), reads=["h", ("rstd", tt)], writes=[("mb", tt)])
                else:
                    P.op("dve", lambda e, tt=tt: e.tensor_copy(out=mb[:, tt, :], in_=h[:, tt, :]),
                         reads=["h"], writes=[("mb", tt)])
            for kc in range(16):
                b = nb()
                pb = banks[b][:, :].bitcast(BF16)
                for tt in range(4):
                    P.op("pe", lambda e, pb=pb, tt=tt, kc=kc: e.transpose(
                        pb[:, tt * 128:(tt + 1) * 128], mb[:, tt, kc * 128:(kc + 1) * 128], ident),
                        reads=[("mb", tt), "cst"], writes=[bres(b)])
                if gi is not None:
                    sc = gcols[:, gi * 16 + kc: gi * 16 + kc + 1]
                else:
                    sc = 1.0
                if kc % 2 == 0:
                    P.op("act", lambda e, pb=pb, kc=kc, sc=sc: e.activation(
                        out=xnT[:, kc, :], in_=pb[:, 0:512], func=AF.Copy, scale=sc),
                        reads=[bres(b), "gcols"], writes=[("xnT", kc)])
                else:
                    P.op("dve", lambda e, pb=pb, kc=kc, sc=sc: e.tensor_scalar(
                        out=xnT[:, kc, :], in0=pb[:, 0:512], scalar1=sc, scalar2=None, op0=ALU.mult),
                        reads=[bres(b), "gcols"], writes=[("xnT", kc)])

        def proj_fm(w2d, ncols, evac):
            for cb in range(ncols // 512):
                bs = [nb() for _ in range(4)]
                for kh in range(2):
                    pt, pr = panel(w2d[kh * 1024:(kh + 1) * 1024, cb * 512:(cb + 1) * 512])
                    for m in range(4):
                        for kc in range(8):
                            kg = kh * 8 + kc
                            mm(banks[bs[m]][:, :], pt[:, kc, m * 128:(m + 1) * 128], xnT[:, kg, :],
                               kg == 0, kg == 15, [pr, ("xnT", kg)], [bres(bs[m])])
                for m in range(4):
                    evac(cb * 4 + m, bs[m])

        def proj_tm_group(lhs_of, lhs_res_of, w2d, K, col0):
            bs = [nb() for _ in range(4)]
            nkp = K // 1024
            for kp in range(nkp):
                pt, pr = panel(w2d[kp * 1024:(kp + 1) * 1024, col0:col0 + 512])
                for tt in range(4):
                    for kc in range(8):
                        kg = kp * 8 + kc
                        mm(banks[bs[tt]][:, :], lhs_of(kg, tt), pt[:, kc, :],
                           kg == 0, kg == K // 128 - 1, [pr, lhs_res_of(kg)], [bres(bs[tt])])
            return bs

        def load_grow(gri):
            P.dma("sp", lambda e: e.dma_start(out=gbc[:, :], in_=grows_d[gri:gri + 1, :].partition_broadcast(128)),
                  writes=["gbc"])

        def post_norm_add():
            for tt in range(4):
                P.op("act", lambda e, tt=tt: e.activation(
                    out=xnT[:, 0:4, :], in_=mb[:, tt, :].rearrange("p (a b) -> p a b", a=4), func=AF.Square,
                    accum_out=stat[:, 16 + tt:17 + tt]),
                    reads=[("mb", tt)], writes=[("xnT", 0), ("xnT", 1), ("xnT", 2), ("xnT", 3), ("stat2", tt)])
                P.op("dve", lambda e, tt=tt: e.tensor_scalar(
                    out=stat[:, 24 + tt:25 + tt], in0=stat[:, 16 + tt:17 + tt], scalar1=1.0 / D, scalar2=EPS,
                    op0=ALU.mult, op1=ALU.add), reads=[("stat2", tt)], writes=[("rstd2", tt)])
                P.op("act", lambda e, tt=tt: e.activation(
                    out=stat[:, 24 + tt:25 + tt], in_=stat[:, 24 + tt:25 + tt], func=AF.Sqrt),
                    reads=[("rstd2", tt)], writes=[("rstd2", tt)])
                P.op("dve", lambda e, tt=tt: e.reciprocal(
                    out=stat[:, 24 + tt:25 + tt], in_=stat[:, 24 + tt:25 + tt]),
                    reads=[("rstd2", tt)], writes=[("rstd2", tt)])
                P.op("dve", lambda e, tt=tt: e.scalar_tensor_tensor(
                    out=mb[:, tt, :], in0=mb[:, tt, :], scalar=stat[:, 24 + tt:25 + tt], in1=gbc[:, :],
                    op0=ALU.mult, op1=ALU.mult), reads=[("mb", tt), ("rstd2", tt), "gbc"], writes=[("mb", tt)])
                P.op("pool", lambda e, tt=tt: e.tensor_tensor(
                    out=h[:, tt, :], in0=h[:, tt, :], in1=mb[:, tt, :], op=ALU.add),
                    reads=["h", ("mb", tt)], writes=["h"])

        def evac_to_mb(fb, bs):
            for tt in range(4):
                b = bs[tt]
                if tt % 2 == 0:
                    P.op("act", lambda e, b=b, tt=tt, fb=fb: e.activation(
                        out=mb[:, tt, fb * 512:(fb + 1) * 512], in_=banks[b][:, :], func=AF.Copy),
                        reads=[bres(b)], writes=[("mb", tt)])
                else:
                    P.op("dve", lambda e, b=b, tt=tt, fb=fb: e.tensor_copy(
                        out=mb[:, tt, fb * 512:(fb + 1) * 512], in_=banks[b][:, :]),
                        reads=[bres(b)], writes=[("mb", tt)])

        hid = carve(0, [128, 64, TT], BF16)

        def mlp(i):
            norm_T(5 + i)
            P.alias([("hid", m) for m in range(64)], ["R"])

            def ev(m, b):
                s_ = sg[state["sg"]]
                sr = ("sg", state["sg"])
                state["sg"] ^= 1
                P.op("act", lambda e, b=b, s_=s_: e.activation(out=s_[:, :], in_=banks[b][:, :], func=AF.Relu),
                     reads=[bres(b)], writes=[sr])
                P.op("dve", lambda e, m=m, s_=s_: e.tensor_tensor(out=hid[:, m, :], in0=s_[:, :], in1=s_[:, :],
                                                                 op=ALU.mult),
                     reads=[sr], writes=[("hid", m)])
            proj_fm(w1_d[i], DFF, ev)
            load_grow(4 + i)
            for fb in range(4):
                bs = proj_tm_group(lambda kg, tt: hid[:, kg, tt * 128:(tt + 1) * 128], lambda kg: ("hid", kg),
                                   w2_d[i], DFF, fb * 512)
                evac_to_mb(fb, bs)
            post_norm_add()
            P.alias(["R"], [("hid", m) for m in range(64)])

        pTt = carve(0, [128, 2, TT], BF16)

        def ple(i, seq, ti):
            norm_T(None, do_norm=False)
            P.alias(["pTt"], ["R"])
            P.dma("pool", lambda e: e.dma_start(
                out=pTt[:, :, :], in_=pT_d[i, seq, :, ti * TT:(ti + 1) * TT].rearrange("(kc p) t -> p kc t", p=128)),
                writes=["pTt"])
            for fb in range(4):
                gb = proj_tm_group(lambda kg, tt: xnT[:, kg, tt * 128:(tt + 1) * 128], lambda kg: ("xnT", kg),
                                   gate_d[i], D, fb * 512)
                eb = [nb() for _ in range(4)]
                pt, pr = panel(plew_d[i][:, fb * 512:(fb + 1) * 512], nkc=2)
                for tt in range(4):
                    for kc in range(2):
                        mm(banks[eb[tt]][:, :], pTt[:, kc, tt * 128:(tt + 1) * 128], pt[:, kc, :],
                           kc == 0, kc == 1, [pr, "pTt"], [bres(eb[tt])])
                for tt in range(4):
                    s_ = sg[state["sg"]]
                    sr = ("sg", state["sg"])
                    state["sg"] ^= 1
                    P.op("act", lambda e, b=gb[tt], s_=s_: e.activation(out=s_[:, :], in_=banks[b][:, :],
                                                                       func=AF.Sigmoid),
                         reads=[bres(gb[tt])], writes=[sr])
                    P.op("dve", lambda e, b=eb[tt], s_=s_: e.tensor_tensor(out=s_[:, :], in0=banks[b][:, :],
                                                                          in1=s_[:, :], op=ALU.mult),
                         reads=[bres(eb[tt]), sr], writes=[sr])
                    P.op("pool", lambda e, tt=tt, fb=fb, s_=s_: e.tensor_tensor(
                        out=h[:, tt, fb * 512:(fb + 1) * 512], in0=h[:, tt, fb * 512:(fb + 1) * 512],
                        in1=s_[:, :], op=ALU.add), reads=["h", sr], writes=["h"])
            P.alias(["R"], ["pTt"])

        def cmul(eng, outr, outi, ar, ai, br, bi, t1, t2, res_in, res_out, tag="", xw=(), ari=None, aii=None):
            TTm = ALU.mult
            T1, T2 = "t1" + tag, "t2" + tag
            P.op(eng, lambda e: e.tensor_tensor(out=t1, in0=ar, in1=br, op=TTm), reads=res_in, writes=[T1] + list(xw))
            P.op(eng, lambda e: e.tensor_tensor(out=t2, in0=ai, in1=bi, op=TTm), reads=res_in, writes=[T2] + list(xw))
            P.op(eng, lambda e: e.tensor_tensor(out=outr, in0=t1, in1=t2, op=ALU.subtract), reads=[T1, T2],
                 writes=res_out)
            ar2 = ar if ari is None else ari
            ai2 = ai if aii is None else aii
            P.op(eng, lambda e: e.tensor_tensor(out=t1, in0=ar2, in1=bi, op=TTm), reads=res_in + res_out,
                 writes=[T1] + list(xw))
            P.op(eng, lambda e: e.tensor_tensor(out=t2, in0=ai2, in1=br, op=TTm), reads=res_in + res_out,
                 writes=[T2] + list(xw))
            P.op(eng, lambda e: e.tensor_tensor(out=outi, in0=t1, in1=t2, op=ALU.add), reads=[T1, T2],
                 writes=res_out)

        def lam_calc(ar, ai, ls, lr, li, fr, fi, tA, tB, tC, tD, tE):
            V = "dve"
            rs = ["lamin", "lamout"]
            ws = ["lamout"]

            def o(fn):
                P.op(V, fn, reads=rs, writes=ws)

            def horner(q, z, coefs):
                o(lambda e: e.tensor_scalar(out=q, in0=z, scalar1=coefs[-1], scalar2=None, op0=ALU.mult))
                for c in reversed(coefs[:-1]):
                    o(lambda e, c=c: e.scalar_tensor_tensor(out=q, in0=q, scalar=c, in1=z, op0=ALU.add, op1=ALU.mult))
            fact = [1.0]
            for k in range(1, 20):
                fact.append(fact[-1] * k)
            o(lambda e: e.tensor_scalar(out=tB, in0=ls, scalar1=0.125, scalar2=None, op0=ALU.mult))
            horner(tA, tB, [1.0 / fact[k] for k in range(1, 13)])
            o(lambda e: e.tensor_scalar(out=tA, in0=tA, scalar1=1.0, scalar2=None, op0=ALU.add))
            for _ in range(3):
                o(lambda e: e.tensor_tensor(out=tA, in0=tA, in1=tA, op=ALU.mult))
            o(lambda e: e.tensor_tensor(out=tC, in0=ar, in1=tA, op=ALU.mult))
            horner(tB, tC, [1.0 / fact[k] for k in range(1, 9)])
            o(lambda e: e.tensor_scalar(out=tB, in0=tB, scalar1=1.0, scalar2=None, op0=ALU.add))
            o(lambda e: e.tensor_tensor(out=tA, in0=ai, in1=tA, op=ALU.mult))
            MAGIC = 12582912.0
            o(lambda e: e.tensor_scalar(out=tC, in0=tA, scalar1=1.0 / (2 * math.pi), scalar2=None, op0=ALU.mult))
            o(lambda e: e.tensor_scalar(out=tC, in0=tC, scalar1=MAGIC, scalar2=None, op0=ALU.add))
            o(lambda e: e.tensor_scalar(out=tC, in0=tC, scalar1=-MAGIC, scalar2=None, op0=ALU.add))
            C1 = 6.28125
            C2 = 2 * math.pi - C1
            o(lambda e: e.scalar_tensor_tensor(out=tA, in0=tC, scalar=-C1, in1=tA, op0=ALU.mult, op1=ALU.add))
            o(lambda e: e.scalar_tensor_tensor(out=tA, in0=tC, scalar=-C2, in1=tA, op0=ALU.mult, op1=ALU.add))
            o(lambda e: e.tensor_scalar(out=tA, in0=tA, scalar1=0.5, scalar2=None, op0=ALU.mult))
            o(lambda e: e.tensor_tensor(out=tC, in0=tA, in1=tA, op=ALU.mult))
            horner(tD, tC, [(-1.0) ** k / fact[2 * k + 1] for k in range(1, 8)])
            o(lambda e: e.scalar_tensor_tensor(out=tD, in0=tD, scalar=1.0, in1=tA, op0=ALU.add, op1=ALU.mult))
            horner(tE, tC, [(-1.0) ** k / fact[2 * k] for k in range(1, 9)])
            o(lambda e: e.tensor_scalar(out=tE, in0=tE, scalar1=1.0, scalar2=None, op0=ALU.add))
            o(lambda e: e.tensor_tensor(out=li, in0=tD, in1=tE, op=ALU.mult))
            o(lambda e: e.scalar_tensor_tensor(out=li, in0=li, scalar=2.0, in1=tB, op0=ALU.mult, op1=ALU.mult))
            o(lambda e: e.tensor_tensor(out=lr, in0=tD, in1=tD, op=ALU.mult))
            o(lambda e: e.tensor_scalar(out=lr, in0=lr, scalar1=-2.0, scalar2=1.0, op0=ALU.mult, op1=ALU.add))
            o(lambda e: e.tensor_tensor(out=lr, in0=lr, in1=tB, op=ALU.mult))
            o(lambda e: e.tensor_tensor(out=tA, in0=ar, in1=ar, op=ALU.mult))
            o(lambda e: e.tensor_tensor(out=tB, in0=ai, in1=ai, op=ALU.mult))
            o(lambda e: e.tensor_tensor(out=tA, in0=tA, in1=tB, op=ALU.add))
            o(lambda e: e.reciprocal(out=tA, in_=tA))
            o(lambda e: e.tensor_scalar(out=tB, in0=lr, scalar1=-1.0, scalar2=None, op0=ALU.add))
            o(lambda e: e.tensor_tensor(out=fr, in0=tB, in1=ar, op=ALU.mult))
            o(lambda e: e.tensor_tensor(out=tC, in0=li, in1=ai, op=ALU.mult))
            o(lambda e: e.tensor_tensor(out=fr, in0=fr, in1=tC, op=ALU.add))
            o(lambda e: e.tensor_tensor(out=fr, in0=fr, in1=tA, op=ALU.mult))
            o(lambda e: e.tensor_tensor(out=fi, in0=li, in1=ar, op=ALU.mult))
            o(lambda e: e.tensor_tensor(out=tC, in0=tB, in1=ai, op=ALU.mult))
            o(lambda e: e.tensor_tensor(out=fi, in0=fi, in1=tC, op=ALU.subtract))
            o(lambda e: e.tensor_tensor(out=fi, in0=fi, in1=tA, op=ALU.mult))

        def s5_prep(l):
            P.alias(["prep", "lamin", "lamout", "t1", "t2", "t1p", "t2p", "G", "Bt", "Vt", "Ot", "Xt", "Ct", "BDt",
                     "wt", "Cb", "dm", "stg", ("Gk", 0), ("Gk", 1)], ["R"])
            KB = 1024
            o = 0
            Gs = carve(o, [128, 2, 8, 64], F32); o += 4096
            Hs = carve(o, [128, 2, 8, 64], F32); o += 4096
            base = o
            a3 = carve(o, [128, 3, 64], F32); o += 768
            sm = carve(o, [128, 9, 64], F32); o += 2304
            tt1 = carve(o, [128, 64], F32); o += 256
            tt2 = carve(o, [128, 64], F32); o += 256
            stg = carve(o, [128, 512], F32); o += 2048
            P.dma("sp", lambda e: e.dma_start(out=a3[:, :, :], in_=a3_d[l].rearrange("k p n -> p k n")),
                  writes=["lamin"])
            lam_calc(a3[:, 0, :], a3[:, 1, :], a3[:, 2, :], sm[:, 0, :], sm[:, 1, :], sm[:, 2, :], sm[:, 3, :],
                     sm[:, 4, :], sm[:, 5, :], sm[:, 6, :], sm[:, 7, :], sm[:, 8, :])
            P.op("dve", lambda e: e.tensor_copy(out=Gs[:, 0, 0, :], in_=sm[:, 2, :]), reads=["lamout"], writes=["G"])
            P.op("dve", lambda e: e.tensor_copy(out=Gs[:, 1, 0, :], in_=sm[:, 3, :]), reads=["lamout"], writes=["G"])
            P.op("dve", lambda e: e.tensor_copy(out=Hs[:, 0, 0, :], in_=sm[:, 0, :]), reads=["lamout"], writes=["G"])
            P.op("dve", lambda e: e.tensor_copy(out=Hs[:, 1, 0, :], in_=sm[:, 1, :]), reads=["lamout"], writes=["G"])
            for k in range(7):
                cmul("dve", Gs[:, 0, k + 1, :], Gs[:, 1, k + 1, :], sm[:, 0, :], sm[:, 1, :], Gs[:, 0, k, :],
                     Gs[:, 1, k, :], tt1, tt2, ["lamout", "G"], ["G"])
                cmul("dve", Hs[:, 0, k + 1, :], Hs[:, 1, k + 1, :], sm[:, 0, :], sm[:, 1, :], Hs[:, 0, k, :],
                     Hs[:, 1, k, :], tt1, tt2, ["lamout", "G"], ["G"])
            P.alias(["Hsn"], [("xnT", 0), ("xnT", 1), ("xnT", 2), ("xnT", 3)])
            Hsn = xnT[:, 0:4, :].rearrange("p a b -> p (a b)").bitcast(F32).rearrange("p (a b c) -> p a b c", a=2, b=8)
            P.op("dve", lambda e: e.tensor_scalar(out=Hsn, in0=Hs, scalar1=-1.0, scalar2=None, op0=ALU.mult),
                 reads=["G"], writes=["Hsn"])
            P.op("dve", lambda e: e.tensor_copy(out=A8[:, l, 0, :], in_=Hs[:, 0, 7, :]), reads=["G"], writes=["A8"])
            P.op("dve", lambda e: e.tensor_copy(out=A8[:, l, 1, :], in_=Hs[:, 1, 7, :]), reads=["G"], writes=["A8"])
            Gs2 = Gs.rearrange("p a b c -> p (a b c)")
            for half in range(2):
                b = nb()
                for q in range(4):
                    blk = half * 4 + q
                    P.op("pe", lambda e, b=b, q=q, blk=blk: e.transpose(
                        banks[b][:, q * 128:(q + 1) * 128], Gs2[:, blk * 128:(blk + 1) * 128], ident_f[:, :]),
                        reads=["G", "ident_f"], writes=[bres(b)])
                P.op("dve", lambda e, b=b: e.tensor_copy(out=stg[:, :], in_=banks[b][:, :]),
                     reads=[bres(b)], writes=["stg"])
                P.dma("sp", lambda e, half=half: e.dma_start(
                    out=Gsd[l, half * 512:(half + 1) * 512, :].rearrange("(q p) c -> p q c", p=128),
                    in_=stg.rearrange("p (q c) -> p q c", q=4)), reads=["stg"], writes=["Gsd"])
            PJ = ["Bt", ("Ct", 0), ("Ct", 1), "Cb", "Vt", ("Xt", 0), ("Xt", 1), "BDt", "dm", ("Gk", 0), ("Gk", 1),
                  "t1", "t2", "t1p", "t2p"]
            P.alias(PJ, ["lamin", "lamout", "stg", "t1", "t2"])
            HN = [("Ot", 0), ("Ot", 1)]
            P.alias(HN, ["h"])
            hb = h.rearrange("p a b -> p (a b)")
            Otb = [hb[:, i * 4096:(i + 1) * 4096].bitcast(BF16).rearrange("p (a b c d) -> p a b c d", a=4, b=8, c=2)
                   for i in range(2)]
            o = base
            Ct2 = [carve(o + i * 4 * KB, [128, 2, 4, 128], F32) for i in range(2)]; o += 8 * KB
            Cb = carve(o, [128, 2, 4, 128], BF16); o += 2 * KB
            BDt = carve(o, [128, 8, 128], BF16); o += 2 * KB
            dm = carve(o, [128, 128], F32); o += KB // 2
            Xt = [carve(o + i * 2 * KB, [128, 2, 4, 128], BF16) for i in range(2)]; o += 4 * KB
            Bt = carve(o, [128, 2, 512], F32); o += 4 * KB
            Gk = [carve(o + i * 4 * KB, [128, 2, 512], F32) for i in range(2)]; o += 8 * KB
            tq1 = carve(o, [128, 512], F32); o += 2 * KB
            tq2 = carve(o, [128, 512], F32); o += 2 * KB
            Vt = carve(o, [128, 4, 8, 2, 128], BF16); o += 16 * KB
            u1 = carve(o, [128, 4, 128], F32); o += 2 * KB
            u2 = carve(o, [128, 4, 128], F32); o += 2 * KB
            for j in range(16):
                Ot = Otb[j % 2]
                otr = ("Ot", j % 2)
                ct_ = Ct2[j % 2]
                cr_ = ("Ct", j % 2)
                P.dma("sp", lambda e, j=j: e.dma_start(
                    out=Bt[:, :, :], in_=bL1_d[l, :, j, :, :].rearrange("k p n -> p k n")), writes=["Bt"])
                P.dma("sp", lambda e, ct_=ct_, j=j: e.dma_start(
                    out=ct_.rearrange("p k q c -> p k (q c)"), in_=cL2_d[l, :, j, :, :].rearrange("k p n -> p k n")),
                    writes=[cr_])
                P.dma("sp", lambda e, j=j: e.dma_start(out=dm[:, :], in_=dmat_d[l, j]), writes=["dm"])
                P.op("dve", lambda e, ct_=ct_: e.tensor_copy(out=Cb[:, 0, :, :], in_=ct_[:, 0, :, :]),
                     reads=[cr_], writes=["Cb"])
                P.op("dve", lambda e, ct_=ct_: e.tensor_scalar(out=Cb[:, 1, :, :], in0=ct_[:, 1, :, :],
                                                               scalar1=-1.0, scalar2=None, op0=ALU.mult),
                     reads=[cr_], writes=["Cb"])
                for k in range(8):
                    s_ = 7 - k
                    gb_ = Gk[k % 2]
                    gr_ = ("Gk", k % 2)
                    xt_ = Xt[k % 2]
                    xr_ = ("Xt", k % 2)
                    P.dma("sp", lambda e, gb_=gb_, j=j, k=k: e.dma_start(
                        out=gb_[:, :, :],
                        in_=Gsd[l].rearrange("(r m) c -> r (m c)", r=2)[:, (k * 64 + 4 * j) * 128:(k * 64 + 4 * j + 4) * 128]
                        .partition_broadcast(128)), reads=["Gsd"], writes=[gr_])
                    v4 = lambda a: a.rearrange("p (q m) -> p q m", q=4)
                    cmul("dve", Vt[:, :, s_, 0, :], Vt[:, :, s_, 1, :], v4(gb_[:, 0, :]), v4(gb_[:, 1, :]),
                         v4(Bt[:, 0, :]), v4(Bt[:, 1, :]), v4(tq1), v4(tq2), [gr_, "Bt"], ["Vt"])
                    tbk = nb()
                    pb = banks[tbk][:, :].bitcast(BF16)
                    for ri in range(2):
                        for q in range(4):
                            P.op("pe", lambda e, pb=pb, ri=ri, q=q, s_=s_: e.transpose(
                                pb[:, (ri * 4 + q) * 128:(ri * 4 + q + 1) * 128], Vt[:, q, s_, ri, :], ident),
                                reads=["Vt", "cst"], writes=[bres(tbk)])
                    P.op("act", lambda e, pb=pb, xt_=xt_: e.activation(
                        out=xt_.rearrange("p r q c -> p (r q c)"), in_=pb[:, :], func=AF.Copy),
                        reads=[bres(tbk)], writes=[xr_])
                    if k % 4 == 0:
                        b = nb()
                    kk = k % 4
                    n = 0
                    for q in range(4):
                        for ri in range(2):
                            mm(banks[b][:, kk * 128:(kk + 1) * 128], xt_[:, ri, q, :], Cb[:, ri, q, :],
                               (kk == 0 and n == 0), (kk == 3 and n == 7), [xr_, "Cb"], [bres(b)])
                            n += 1
                    if kk == 3:
                        k4 = k // 4
                        if k4 == 0:
                            P.op("dve", lambda e, b=b: e.tensor_tensor(
                                out=banks[b][:, 0:128], in0=banks[b][:, 0:128], in1=dm[:, :], op=ALU.add),
                                reads=[bres(b), "dm"], writes=[bres(b)])
                        P.op("act", lambda e, b=b, k4=k4: e.activation(
                            out=BDt[:, 4 * k4:4 * k4 + 4, :], in_=banks[b][:, :].rearrange("p (k c) -> p k c", k=4),
                            func=AF.Copy), reads=[bres(b)], writes=["BDt"])
                P.dma("sp", lambda e, j=j: e.dma_start(
                    out=Vd[l, j], in_=Vt.rearrange("p a b c d -> p (a b c d)")), reads=["Vt"], writes=["Vd"])
                P.dma("sp", lambda e, j=j: e.dma_start(
                    out=BDd[l, j], in_=BDt.rearrange("p a b -> p (a b)")), reads=["BDt"], writes=["BDd"])

                def bc(small, ri, k, j=j):
                    return small[:, ri, k, 4 * j:4 * j + 4].unsqueeze(2).to_broadcast([128, 4, 128])
                for k in range(8):
                    cmul("pool", Ot[:, :, k, 0, :], Ot[:, :, k, 1, :], bc(Hs, 0, k), bc(Hs, 1, k), ct_[:, 0, :, :],
                         ct_[:, 1, :, :], u1, u2, ["G", "Hsn", cr_], [otr], tag="p", ari=bc(Hsn, 0, k), aii=bc(Hsn, 1, k))
                P.dma("sp", lambda e, j=j, Ot=Ot: e.dma_start(
                    out=Od[l, j], in_=Ot.rearrange("p a b c d -> p (a b c d)")), reads=[otr], writes=["Od"])
            P.alias(["h"], HN)
            P.alias([("xnT", 0), ("xnT", 1), ("xnT", 2), ("xnT", 3)], ["Hsn"])
            P.alias(["R"], ["prep", "lamin", "lamout", "G", "stg"] + PJ)

        def s5(l, first_tile):
            KB = 1024
            o = 0
            uT = carve(o, [128, 16, TT], BF16); o += 16 * KB
            VO = [carve(o + k * 4 * KB, [128, 8, 2, 128], BF16) for k in range(4)]; o += 16 * KB
            BDb = [carve(o + k * 2 * KB, [128, 8, 128], BF16) for k in range(2)]; o += 4 * KB
            Sb = carve(o, [128, NCH, 2, 32], F32); o += 16 * KB
            Xp = carve(o, [128, 2, 32, NCH], BF16); o += 8 * KB
            sc = carve(o, [128, 2, 2, 32], F32); o += KB
            names = ([("fa", j) for j in range(16)] + [("VO", k) for k in range(4)]
                     + [("BDb", k) for k in range(2)] + [("Sbc", c) for c in range(NCH)] + ["Xp", "st", "su0", "su1"])
            sball = [("Sbc", c) for c in range(NCH)]
            norm_T(l)
            P.alias(names, ["R"])

            def ev_u(m, b):
                dst = uT[:, m, :].rearrange("p (s c) -> p c s", s=T8)
                src = banks[b][:, :].rearrange("p (c s) -> p c s", s=T8)
                if m % 2 == 0:
                    P.op("act", lambda e, dst=dst, src=src: e.activation(out=dst, in_=src, func=AF.Copy),
                         reads=[bres(b)], writes=[("fa", m)])
                else:
                    P.op("dve", lambda e, dst=dst, src=src: e.tensor_copy(out=dst, in_=src),
                         reads=[bres(b)], writes=[("fa", m)])
            proj_fm(w_in_d[l], D, ev_u)
            if first_tile:
                P.op("dve", lambda e: e.memset(Xst[:, l, :, :], 0.0), writes=[("Xst", l)])
            cnt = {"vo": 0, "bd": 0}
            for half in range(2):
                for jj in range(8):
                    j = half * 8 + jj
                    b = nb()
                    for q in range(4):
                        k = cnt["vo"] % 4
                        cnt["vo"] += 1
                        P.dma("sp", lambda e, k=k, j=j, q=q: e.dma_start(
                            out=VO[k].rearrange("p a b c -> p (a b c)"), in_=Vd[l, j][:, q * 2048:(q + 1) * 2048]),
                            reads=["Vd"], writes=[("VO", k)])
                        for ri in range(2):
                            for s in range(T8):
                                mm(banks[b][:, (q * 2 + ri) * NCH:(q * 2 + ri + 1) * NCH], VO[k][:, s, ri, :],
                                   uT[:, j, s * NCH:(s + 1) * NCH], (q == 0 and ri == 0 and s == 0),
                                   (q == 3 and ri == 1 and s == T8 - 1),
                                   [("VO", k), ("fa", j)], [bres(b)])
                    P.op("act", lambda e, b=b, jj=jj: e.activation(
                        out=Sb[:, :, :, 4 * jj:4 * jj + 4].rearrange("p c r q -> p q r c"),
                        in_=banks[b][:, :].rearrange("p (q r c) -> p q r c", q=4, r=2), func=AF.Copy),
                        reads=[bres(b)], writes=sball)
                Ar2 = A8[:, l, 0, 32 * half:32 * half + 32].unsqueeze(1).to_broadcast([128, 2, 32])
                Ai = A8[:, l, 1, 32 * half:32 * half + 32]
                for c in range(NCH):
                    if c == 0:
                        Xprev = Xst[:, l, :, 32 * half:32 * half + 32]
                        pres = ("Xst", l)
                    else:
                        Xprev = Sb[:, c - 1, :, :]
                        pres = ("Sbc", c - 1)
                    P.op("dve", lambda e, Xprev=Xprev, Ar2=Ar2: e.tensor_tensor(out=sc[:, 0, :, :], in0=Xprev, in1=Ar2,
                                                                                op=ALU.mult),
                         reads=[pres, "A8"], writes=["st"])
                    P.op("dve", lambda e, Xprev=Xprev, Ai=Ai: e.scalar_tensor_tensor(
                        out=sc[:, 1, 0, :], in0=Xprev[:, 1, :], scalar=-1.0, in1=Ai, op0=ALU.mult, op1=ALU.mult),
                        reads=[pres, "A8"], writes=["su0"])
                    P.op("dve", lambda e, Xprev=Xprev, Ai=Ai: e.tensor_tensor(out=sc[:, 1, 1, :], in0=Xprev[:, 0, :],
                                                                              in1=Ai, op=ALU.mult),
                         reads=[pres, "A8"], writes=["su1"])
                    P.op("dve", lambda e, c=c: e.tensor_tensor(out=Sb[:, c, :, :], in0=Sb[:, c, :, :],
                                                               in1=sc[:, 0, :, :], op=ALU.add),
                         reads=["st", ("Sbc", c)], writes=[("Sbc", c)])
                    P.op("dve", lambda e, c=c: e.tensor_tensor(out=Sb[:, c, :, :], in0=Sb[:, c, :, :],
                                                               in1=sc[:, 1, :, :], op=ALU.add),
                         reads=["su0", "su1", ("Sbc", c)], writes=[("Sbc", c)])
                P.op("act", lambda e, half=half: e.activation(out=Xp[:, :, :, 0],
                                                              in_=Xst[:, l, :, 32 * half:32 * half + 32],
                                                              func=AF.Copy), reads=[("Xst", l)], writes=["Xp"])
                P.op("act", lambda e: e.activation(out=Xp[:, :, :, 1:NCH],
                                                   in_=Sb[:, 0:NCH - 1, :, :].rearrange("p c r q -> p r q c"),
                                                   func=AF.Copy), reads=sball, writes=["Xp"])
                P.op("dve", lambda e, half=half: e.tensor_copy(out=Xst[:, l, :, 32 * half:32 * half + 32],
                                                               in_=Sb[:, NCH - 1, :, :]),
                     reads=sball + ["Xp"], writes=[("Xst", l)])
                for jj in range(8):
                    j = half * 8 + jj
                    kb = cnt["bd"] % 2
                    cnt["bd"] += 1
                    P.dma("sp", lambda e, kb=kb, j=j: e.dma_start(
                        out=BDb[kb].rearrange("p a b -> p (a b)"), in_=BDd[l, j]),
                        reads=["BDd"], writes=[("BDb", kb)])
                    b = nb()
                    for t in range(T8):
                        for s in range(t + 1):
                            mm(banks[b][:, t * NCH:(t + 1) * NCH], BDb[kb][:, t - s, :], uT[:, j, s * NCH:(s + 1) * NCH],
                               (t == 0 and s == 0), False, [("BDb", kb), ("fa", j)], [bres(b)])
                    for q in range(4):
                        k = cnt["vo"] % 4
                        cnt["vo"] += 1
                        P.dma("sp", lambda e, k=k, j=j, q=q: e.dma_start(
                            out=VO[k].rearrange("p a b c -> p (a b c)"), in_=Od[l, j][:, q * 2048:(q + 1) * 2048]),
                            reads=["Od"], writes=[("VO", k)])
                        for t in range(T8):
                            for ri in range(2):
                                mm(banks[b][:, t * NCH:(t + 1) * NCH], VO[k][:, t, ri, :],
                                   Xp[:, ri, 4 * jj + q, :], False, (q == 3 and ri == 1 and t == T8 - 1),
                                   [("VO", k), "Xp"], [bres(b)])
                    s_ = sg[state["sg"]]
                    sr = ("sg", state["sg"])
                    state["sg"] ^= 1
                    s2 = sg[state["sg"]]
                    sr2 = ("sg", state["sg"])
                    state["sg"] ^= 1
                    P.op("act", lambda e, b=b, s_=s_: e.activation(out=s_[:, :], in_=banks[b][:, :], func=AF.Square),
                         reads=[bres(b)], writes=[sr])
                    P.op("dve", lambda e, s_=s_: e.tensor_scalar(out=s_[:, :], in0=s_[:, :], scalar1=0.044715,
                                                                 scalar2=1.0, op0=ALU.mult, op1=ALU.add),
                         reads=[sr], writes=[sr])
                    P.op("dve", lambda e, b=b, s_=s_: e.tensor_tensor(out=s_[:, :], in0=s_[:, :], in1=banks[b][:, :],
                                                                      op=ALU.mult), reads=[sr, bres(b)], writes=[sr])
                    P.op("act", lambda e, s_=s_, s2=s2: e.activation(out=s2[:, :], in_=s_[:, :], func=AF.Sigmoid,
                                                                     scale=1.5957691216057308),
                         reads=[sr], writes=[sr2])
                    P.op("dve", lambda e, b=b, s2=s2, j=j: e.tensor_tensor(
                        out=uT[:, j, :].rearrange("p (c t) -> p t c", t=T8),
                        in0=s2[:, :].rearrange("p (t c) -> p t c", t=T8),
                        in1=banks[b][:, :].rearrange("p (t c) -> p t c", t=T8), op=ALU.mult),
                        reads=[sr2, bres(b)], writes=[("fa", j)])
            load_grow(l)
            for fb in range(4):
                vb_ = proj_tm_group(lambda kg, tt: uT[:, kg, tt * 128:(tt + 1) * 128], lambda kg: ("fa", kg),
                                    w_glu_d[l], D, fb * 512)
                gb_ = proj_tm_group(lambda kg, tt: uT[:, kg, tt * 128:(tt + 1) * 128], lambda kg: ("fa", kg),
                                    w_glu_d[l], D, D + fb * 512)
                for tt in range(4):
                    s_ = sg[state["sg"]]
                    sr = ("sg", state["sg"])
                    state["sg"] ^= 1
                    P.op("act", lambda e, b=gb_[tt], s_=s_: e.activation(out=s_[:, :], in_=banks[b][:, :],
                                                                        func=AF.Sigmoid),
                         reads=[bres(gb_[tt])], writes=[sr])
                    P.op("dve", lambda e, b=vb_[tt], s_=s_, tt=tt, fb=fb: e.tensor_tensor(
                        out=mb[:, tt, fb * 512:(fb + 1) * 512], in0=banks[b][:, :], in1=s_[:, :], op=ALU.mult),
                        reads=[bres(vb_[tt]), sr], writes=[("mb", tt)])
            post_norm_add()
            P.alias(["R"], names)

        def kv_proj(ti):
            norm_T(2)

            def ev_k(m, b):
                P.op("act", lambda e, b=b, m=m: e.activation(out=kTc[:, m, ti * TT:(ti + 1) * TT], in_=banks[b][:, :],
                                                            func=AF.Copy), reads=[bres(b)], writes=["kTc"])
            proj_fm(w_k_d, 512, ev_k)
            bs = proj_tm_group(lambda kg, tt: xnT[:, kg, tt * 128:(tt + 1) * 128], lambda kg: ("xnT", kg),
                               w_v_d, D, 0)
            for tt in range(4):
                P.op("dve", lambda e, b=bs[tt], tt=tt: e.tensor_copy(out=vvc[:, ti * 4 + tt, :], in_=banks[b][:, :]),
                     reads=[bres(bs[tt])], writes=["vvc"])

        def attn(jb, ti):
            KB = 1024
            o = 0
            qT = carve(o, [128, 16, TT], BF16); o += 16 * KB
            eb = [carve(o + k * 2 * KB, [128, 512], F32) for k in range(4)]; o += 8 * KB
            Lp = [carve(o + k * KB, [128, 512], BF16) for k in range(8)]; o += 8 * KB
            tb = [carve(o + k * 2 * KB, [128, 512], F32) for k in range(8)]; o += 16 * KB
            wT = [carve(o + k * KB, [128, 512], BF16) for k in range(6)]; o += 6 * KB
            names = ([("fa", j) for j in range(16)] + [("eb", k) for k in range(4)] + [("Lp", k) for k in range(8)]
                     + [("tb", k) for k in range(8)] + [("wT", k) for k in range(6)])
            norm_T(3 + jb)
            P.alias(names, ["R"])

            def ev_q(m, b):
                if m % 2 == 0:
                    P.op("act", lambda e, b=b, m=m: e.activation(out=qT[:, m, :], in_=banks[b][:, :], func=AF.Copy),
                         reads=[bres(b)], writes=[("fa", m)])
                else:
                    P.op("dve", lambda e, b=b, m=m: e.tensor_copy(out=qT[:, m, :], in_=banks[b][:, :]),
                         reads=[bres(b)], writes=[("fa", m)])
            proj_fm(w_q_d[jb], D, ev_q)
            streams = [[], []]
            for qb in range(4):
                nkb = ti * 4 + qb + 1
                for kvh in range(4):
                    st_ = kvh % 2
                    for idx, kb in enumerate(range(nkb - 1, -1, -1)):
                        streams[st_].append(dict(qb=qb, kvh=kvh, kb=kb, first=(idx == 0), last=(kb == 0),
                                                 diag=(kb == nkb - 1), acc=st_, ob=2 + st_, s=st_,
                                                 i=len(streams[st_])))

            def qview(u):
                return qT[:, 4 * u["kvh"]:4 * u["kvh"] + 4, u["qb"] * 128:(u["qb"] + 1) * 128]

            def hres_of(u):
                return [("fa", 4 * u["kvh"] + g) for g in range(4)]

            def stageA(u):
                i = u["i"]
                zb = 4 + 2 * u["s"] + i % 2
                e2, l4 = 2 * u["s"] + i % 2, 4 * u["s"] + i % 4
                kb, kvh = u["kb"], u["kvh"]
                mm(banks[zb][:, :].rearrange("p (g q) -> p g q", g=4),
                   kTc[:, kvh, kb * 128:(kb + 1) * 128], qview(u), True, True, ["kTc"] + hres_of(u), [bres(zb)])
                P.op("act", lambda e, zb=zb, e2=e2: e.activation(out=eb[e2][:, :], in_=banks[zb][:, :],
                                                                 func=AF.Exp, scale=SCALE),
                     reads=[bres(zb)], writes=[("eb", e2)])
                P.op("act", lambda e, e2=e2, l4=l4: e.activation(out=Lp[l4][:, :], in_=eb[e2][:, :], func=AF.Ln,
                                                                 bias=1.0),
                     reads=[("eb", e2)], writes=[("Lp", l4)])
                if u["diag"]:
                    P.op("dve", lambda e, l4=l4: e.tensor_tensor(out=Lp[l4][:, :], in0=Lp[l4][:, :],
                                                                 in1=maskd, op=ALU.mult),
                         reads=[("Lp", l4), "cst"], writes=[("Lp", l4)])
                P.op("dve", lambda e, zb=zb, l4=l4: e.scalar_tensor_tensor(
                    out=tb[l4][:, :], in0=banks[zb][:, :], scalar=SCALE, in1=Lp[l4][:, :],
                    op0=ALU.mult, op1=ALU.subtract), reads=[bres(zb), ("Lp", l4)], writes=[("tb", l4)])

            def stageB(u):
                i = u["i"]
                l4, w2_ = 4 * u["s"] + i % 4, 3 * u["s"] + i % 3
                acc = u["acc"]
                mm(banks[acc][:, :], Umat, Lp[l4][:, :], u["first"], False, ["cst", ("Lp", l4)], [bres(acc)])
                P.op("dve", lambda e, acc=acc, l4=l4: e.tensor_tensor(
                    out=tb[l4][:, :], in0=tb[l4][:, :], in1=banks[acc][:, :], op=ALU.subtract),
                    reads=[("tb", l4), bres(acc)], writes=[("tb", l4)])
                if not u["last"]:
                    mm(banks[acc][:, :], Lomat, Lp[l4][:, :], False, False, ["cst", ("Lp", l4)], [bres(acc)])
                P.op("act", lambda e, l4=l4, w2_=w2_: e.activation(out=wT[w2_][:, :], in_=tb[l4][:, :], func=AF.Exp),
                     reads=[("tb", l4)], writes=[("wT", w2_)])
                if u["diag"]:
                    P.op("dve", lambda e, w2_=w2_: e.tensor_tensor(out=wT[w2_][:, :], in0=wT[w2_][:, :],
                                                                   in1=maskd, op=ALU.mult),
                         reads=[("wT", w2_), "cst"], writes=[("wT", w2_)])

            def stageC(u):
                i = u["i"]
                w2_ = 3 * u["s"] + i % 3
                ob = u["ob"]
                kb, kvh = u["kb"], u["kvh"]
                mm(banks[ob][:, :], vvc[:, kb, kvh * 128:(kvh + 1) * 128], wT[w2_][:, :], u["first"], u["last"],
                   ["vvc", ("wT", w2_)], [bres(ob)])
                if u["last"]:
                    qv = qview(u)
                    P.op("act", lambda e, ob=ob, qv=qv: e.activation(
                        out=qv, in_=banks[ob][:, :].rearrange("p (g q) -> p g q", g=4), func=AF.Copy),
                        reads=[bres(ob)], writes=hres_of(u))
            nu = len(streams[0])
            for i in range(min(2, nu)):
                for st_ in range(2):
                    stageA(streams[st_][i])
            for i in range(nu):
                for st_ in range(2):
                    stageB(streams[st_][i])
                if i + 2 < nu:
                    for st_ in range(2):
                        stageA(streams[st_][i + 2])
                if i >= 1:
                    for st_ in range(2):
                        stageC(streams[st_][i - 1])
            for st_ in range(2):
                stageC(streams[st_][nu - 1])
            load_grow(2 + jb)
            for fb in range(4):
                bs = proj_tm_group(lambda kg, tt: qT[:, kg, tt * 128:(tt + 1) * 128], lambda kg: ("fa", kg),
                                   w_o_d[jb], D, fb * 512)
                evac_to_mb(fb, bs)
            post_norm_add()
            P.alias(["R"], names)

        def emit_all():
            load_const()
            P.op("dve", lambda e: e.memset(stat[:, :], 0.0), writes=["statinit"])
            for l in layers_a:
                s5_prep(l)
            for seq in range(NSEQ):
                for ti in range(NT):
                    P.dma("sp", lambda e, seq=seq, ti=ti: e.dma_start(
                        out=h[:, :, :], in_=x_d[seq, ti * TT:(ti + 1) * TT, :].rearrange("(tt p) d -> p tt d", p=128)),
                        writes=["h"])
                    for (kind, i) in plan:
                        if kind == "mix":
                            if i < 2:
                                s5(i, ti == 0)
                            else:
                                if i == 2:
                                    kv_proj(ti)
                                attn(i - 2, ti)
                        elif kind == "mlp":
                            mlp(i)
                        elif kind == "ple":
                            ple(i, seq, ti)
                    P.dma("sp", lambda e, seq=seq, ti=ti: e.dma_start(
                        out=out_d[seq, ti * TT:(ti + 1) * TT, :].rearrange("(tt p) d -> p tt d", p=128), in_=h[:, :, :]),
                        reads=["h"], final=True)

        emit_all()
        P = Prog(nc)
        state.update({"bank": 0, "pan": 0, "sg": 0, "att": 0, "zb": 0, "recording": False, "issued": 0})
        emit_all()
        P.emit(st)
        build.stats = P.stats
    return nc


def host_layouts(inp):
    f = np.float32
    g = {}
    gl = [inp["a_norm_pre"][0], inp["a_norm_pre"][1], inp["kv_norm"], inp["b_norm_pre"][0], inp["b_norm_pre"][1],
          inp["mlp_norm_pre"][0], inp["mlp_norm_pre"][1], inp["mlp_norm_pre"][2], inp["mlp_norm_pre"][3]]
    gc = np.stack([np.asarray(v, f).reshape(16, 128).T for v in gl], axis=1)
    g["gcols"] = np.ascontiguousarray(gc.reshape(128, 9 * 16))
    g["grows"] = np.ascontiguousarray(np.stack([inp["a_norm_post"][0], inp["a_norm_post"][1], inp["b_norm_post"][0],
                                                inp["b_norm_post"][1], inp["mlp_norm_post"][0],
                                                inp["mlp_norm_post"][1], inp["mlp_norm_post"][2],
                                                inp["mlp_norm_post"][3]]).astype(f))
    ar = np.asarray(inp["ssm_a_re"], f)
    ai = np.asarray(inp["ssm_a_im"], f)
    ls = np.broadcast_to(np.asarray(inp["ssm_log_step"], f)[:, :, None], ar.shape)
    a_all = np.stack([ar, ai, ls], axis=1)
    g["aL1"] = np.ascontiguousarray(a_all.reshape(2, 3, 16, 512))
    g["a3"] = np.ascontiguousarray(a_all.reshape(2, 3, 64, 2, 64).transpose(0, 1, 3, 4, 2).reshape(2, 3, 128, 64))
    b = np.stack([np.asarray(inp["ssm_b_re"], f), np.asarray(inp["ssm_b_im"], f)], axis=1)
    c = np.stack([np.asarray(inp["ssm_c_re"], f), np.asarray(inp["ssm_c_im"], f)], axis=1)
    bL1 = np.zeros((2, 2, 16, 8, 16, 8, 64), f)
    bL2 = np.zeros((2, 2, 16, 2, 64, 4, 8, 16), f)
    cL2 = np.zeros((2, 2, 16, 2, 64, 4, 8, 16), f)
    bj = b.reshape(2, 2, 16, 8, 64, 16)
    cj = c.reshape(2, 2, 16, 8, 16, 64)
    for g8 in range(8):
        bL1[:, :, :, g8, :, g8, :] = bj[:, :, :, g8].transpose(0, 1, 2, 4, 3)
        q, g2 = g8 // 2, g8 % 2
        bL2[:, :, :, g2, :, q, g8, :] = bj[:, :, :, g8]
        cL2[:, :, :, g2, :, q, g8, :] = cj[:, :, :, g8].transpose(0, 1, 2, 4, 3)
    g["bL1"] = bL1.reshape(2, 2, 16, 128, 512)
    g["bL2"] = bL2.reshape(2, 2, 16, 128, 512)
    g["cL2"] = cL2.reshape(2, 2, 16, 128, 512)
    d = np.asarray(inp["ssm_d"], f).reshape(2, 16, 128)
    dm = np.zeros((2, 16, 128, 128), f)
    idx = np.arange(128)
    dm[:, :, idx, idx] = d
    g["dmat"] = dm
    cst = np.zeros((128, 1024), f)
    cst[idx, idx] = 1.0
    jj, ss = np.meshgrid(idx, idx, indexing="ij")
    cst[:, 128:256] = (jj > ss)
    cst[:, 256:384] = (jj <= ss)
    cst[:, 384:896] = np.tile((jj < ss).astype(f), (1, 4))
    g["consts"] = cst
    for k in ("ssm_w_in", "ssm_w_glu", "w_k", "w_v", "w_q", "w_o", "mlp_w1", "mlp_w2", "ple_w", "ple_gate"):
        g[k] = np.ascontiguousarray(np.asarray(inp[k], f))
    return g


def run(inp, S, nseq_total, ncores, plan):
    shared = host_layouts(inp)
    x = np.asarray(inp["x"], np.float32)
    p = np.asarray(inp["p"], np.float32)
    nseq = nseq_total // ncores
    nc = build(S, nseq, plan)
    in_maps = []
    for c in range(ncores):
        m = dict(shared)
        m["x"] = np.ascontiguousarray(x[c * nseq:(c + 1) * nseq])
        m["pT"] = np.ascontiguousarray(p[:, c * nseq:(c + 1) * nseq].transpose(0, 1, 3, 2))
        in_maps.append(m)
    res = run_bass_kernel_spmd(nc, in_maps, core_ids=list(range(ncores)))
    return np.concatenate([r["out"] for r in res.results], axis=0)


def kernel(**inputs):
    return run(inputs, 2048, 16, 8, full_plan())
```

```python
import contextlib
import math
import numpy as np
import concourse.bass as bass
import concourse.mybir as mybir
from concourse.bass_utils import run_bass_kernel_spmd

F32 = mybir.dt.float32
BF16 = mybir.dt.bfloat16
ALU = mybir.AluOpType
AF = mybir.ActivationFunctionType

D = 2048
DFF = 8192
DEPTH = 4
TT = 512
T8 = 8
NCH = TT // T8
EPS = 1e-6
SCALE = 128 ** -0.5


class Op:
    __slots__ = ("eng", "fn", "reads", "writes", "dma", "deps", "signal", "semval")

    def __init__(self, eng, fn, reads, writes, dma):
        self.eng = eng
        self.fn = fn
        self.reads = reads
        self.writes = writes
        self.dma = dma
        self.deps = []
        self.signal = False
        self.semval = None


class Prog:
    SAME_ENGINE_SYNC = ("dve", "pool")
    NDMA = {"sp": 24, "pool": 12, "act": 4}
    ROT = 30000

    def __init__(self, nc):
        self.nc = nc
        self.ops = []
        self.final_dma = []

    def engine(self, name):
        nc = self.nc
        return {"pe": nc.tensor, "act": nc.scalar, "dve": nc.vector, "pool": nc.gpsimd, "sp": nc.sync}[name]

    def op(self, eng, fn, reads=(), writes=()):
        o = Op(eng, fn, tuple(reads), tuple(writes), False)
        self.ops.append(o)
        return o

    def dma(self, eng, fn, reads=(), writes=(), final=False):
        o = Op(eng, fn, tuple(reads), tuple(writes), True)
        self.ops.append(o)
        if final:
            self.final_dma.append(o)
        return o

    def alias(self, new, old):
        self.ops.append(("alias", tuple(new), tuple(old)))

    def emit(self, stack):
        nc = self.nc
        ops = self.ops
        last_write = {}
        reads_since = {}
        engs = ("pe", "act", "dve", "pool", "sp")
        seen = {e: {} for e in engs}
        seen_dma = {e: set() for e in engs}
        for i, o in enumerate(ops):
            if isinstance(o, tuple):
                _, new, old = o
                pend = set()
                for r in old:
                    if r in last_write:
                        pend.add(last_write[r])
                    pend.update(reads_since.get(r, ()))
                for r in new:
                    last_write.pop(r, None)
                    reads_since[r] = sorted(pend)
                continue
            deps = set()
            for r in o.reads:
                lw = last_write.get(r)
                if lw is not None:
                    deps.add(lw)
            for w in o.writes:
                lw = last_write.get(w)
                if lw is not None:
                    deps.add(lw)
                deps.update(reads_since.get(w, ()))
            deps.discard(i)
            best = {}
            final = []
            for d in deps:
                od = ops[d]
                if od.dma:
                    if d in seen_dma[o.eng]:
                        continue
                    seen_dma[o.eng].add(d)
                    final.append(d)
                    od.signal = True
                else:
                    if od.eng == o.eng and not o.dma:
                        if od.eng not in self.SAME_ENGINE_SYNC:
                            continue
                    if seen[o.eng].get(od.eng, -1) >= d:
                        continue
                    if best.get(od.eng, -1) < d:
                        best[od.eng] = d
            for e, d in best.items():
                seen[o.eng][e] = d
                ops[d].signal = True
                final.append(d)
            o.deps = final
            for r in o.reads:
                reads_since.setdefault(r, []).append(i)
            for w in o.writes:
                last_write[w] = i
                reads_since[w] = []
        for o in self.final_dma:
            o.signal = True
        nsig = {e: 0 for e in engs}
        for o in ops:
            if not isinstance(o, tuple) and o.signal and not o.dma:
                nsig[o.eng] += 1
        csems = {}
        for e in engs:
            n = nsig[e] // self.ROT + 1
            csems[e] = [stack.enter_context(nc.semaphore(f"c_{e}_{k}")) for k in range(n)]
        dsems = {e: [stack.enter_context(nc.semaphore(f"d_{e}_{k}")) for k in range(n)]
                 for e, n in self.NDMA.items()}
        ccount = {e: 0 for e in engs}
        dcount = {e: 0 for e in self.NDMA}
        for o in ops:
            if isinstance(o, tuple):
                continue
            eng = self.engine(o.eng)
            for d in o.deps:
                sem, val = ops[d].semval
                eng.wait_ge(sem, val)
            if o.dma:
                n = dcount[o.eng]
                dcount[o.eng] += 1
                pool = dsems[o.eng]
                slot = n % len(pool)
                cnt = n // len(pool) + 1
                if cnt > 1:
                    eng.wait_ge(pool[slot], 16 * (cnt - 1))
                ins = o.fn(eng)
                ins.then_inc(pool[slot], 16)
                o.semval = (pool[slot], 16 * cnt)
            else:
                ins = o.fn(eng)
                if o.signal:
                    c = ccount[o.eng]
                    ccount[o.eng] += 1
                    sem = csems[o.eng][c // self.ROT]
                    ins.then_inc(sem, 1)
                    o.semval = (sem, c % self.ROT + 1)
        sp = nc.sync
        for o in self.final_dma:
            sem, val = o.semval
            sp.wait_ge(sem, val)
        self.stats = dict(n_ops=len(ops), nsig=nsig, ndma=dcount)


def full_plan():
    plan = []
    for i in range(DEPTH):
        plan += [("mix", i), ("mlp", i), ("ple", i)]
    return plan


def build(S, NSEQ, plan):
    NT = S // TT
    nc = bass.Bass("TRN2", target_bir_lowering=False)
    layers_a = sorted({i for (k, i) in plan if k == "mix" and i < 2})

    def din(name, shape):
        return nc.dram_tensor(name, list(shape), F32, kind="ExternalInput").ap()

    x_d = din("x", [NSEQ, S, D])
    pT_d = din("pT", [DEPTH, NSEQ, 256, S])
    w_in_d = din("ssm_w_in", [2, D, D])
    w_glu_d = din("ssm_w_glu", [2, D, 2 * D])
    w_k_d = din("w_k", [D, 512])
    w_v_d = din("w_v", [D, 512])
    w_q_d = din("w_q", [2, D, D])
    w_o_d = din("w_o", [2, D, D])
    w1_d = din("mlp_w1", [DEPTH, D, DFF])
    w2_d = din("mlp_w2", [DEPTH, DFF, D])
    plew_d = din("ple_w", [DEPTH, 256, D])
    gate_d = din("ple_gate", [DEPTH, D, D])
    gcols_d = din("gcols", [128, 9 * 16])
    grows_d = din("grows", [8, D])
    aL1_d = din("aL1", [2, 3, 16, 512])
    bL1_d = din("bL1", [2, 2, 16, 128, 512])
    a3_d = din("a3", [2, 3, 128, 64])
    bL2_d = din("bL2", [2, 2, 16, 128, 512])
    cL2_d = din("cL2", [2, 2, 16, 128, 512])
    dmat_d = din("dmat", [2, 16, 128, 128])
    cst_d = din("consts", [128, 1024])
    out_d = nc.dram_tensor("out", [NSEQ, S, D], F32, kind="ExternalOutput").ap()
    Vd = nc.dram_tensor("Vd", [2, 16, 128, 8192], BF16, kind="Internal").ap()
    Od = nc.dram_tensor("Od", [2, 16, 128, 8192], BF16, kind="Internal").ap()
    BDd = nc.dram_tensor("BDd", [2, 16, 128, 1024], BF16, kind="Internal").ap()
    Gsd = nc.dram_tensor("Gsd", [2, 1024, 128], F32, kind="Internal").ap()

    st = contextlib.ExitStack()
    with st:
        def sb(name, shape, dt):
            return st.enter_context(nc.sbuf_tensor("s_" + name, list(shape), dt))

        h = sb("h", [128, 4, D], F32)
        xnT = sb("xnT", [128, 16, TT], BF16)
        pans = [sb(f"pan{i}", [128, 8, 512], BF16) for i in range(3)]
        mb = sb("mb", [128, 4, D], BF16)
        gbc = sb("gbc", [128, D], F32)
        kTc = sb("kTc", [128, 4, S], BF16)
        vvc = sb("vvc", [128, S // 128, 512], BF16)
        sg = [sb(f"sg{i}", [128, 512], F32) for i in range(2)]
        gcols = sb("gcols", [128, 9 * 16], F32)
        cst = sb("cst", [128, 1024], BF16)
        ident_f = sb("ident_f", [128, 128], F32)
        stat = sb("stat", [128, 64], F32)
        Xst = sb("Xst", [128, 2, 2, 64], F32)
        A8 = sb("A8", [128, 2, 2, 64], F32)
        R = sb("R", [128, 32768], BF16)
        banks = [st.enter_context(nc.psum_tensor(f"ps{i}", [128, 512], F32)) for i in range(8)]

        ident = cst[:, 0:128]
        Umat = cst[:, 128:256]
        Lomat = cst[:, 256:384]
        maskd = cst[:, 384:896]

        HALL = [("h", 0), ("h", 1), ("h", 2), ("h", 3)]
        P = Prog(nc)
        state = {"bank": 0, "pan": 0, "sg": 0, "att": 0, "zb": 0, "recording": True, "plog": [], "issued": 0}

        def nb():
            b = state["bank"]
            state["bank"] = (b + 1) % 8
            return b

        def bres(b):
            return ("ps", b)

        def carve(off, shape, dt):
            n = int(np.prod(shape[1:]))
            if dt == F32:
                v = R[:, off // 2: off // 2 + 2 * n].bitcast(F32)
            else:
                v = R[:, off // 2: off // 2 + n]
            if len(shape) == 3:
                v = v.rearrange("p (a b) -> p a b", a=shape[1])
            elif len(shape) == 4:
                v = v.rearrange("p (a b c) -> p a b c", a=shape[1], b=shape[2])
            elif len(shape) == 5:
                v = v.rearrange("p (a b c d) -> p a b c d", a=shape[1], b=shape[2], c=shape[3])
            return v

        def panel(src, nkc=8):
            n = state["pan"]
            state["pan"] = n + 1
            if state["recording"]:
                state["plog"].append((src, nkc))
                return pans[n % 3], ("pan", n % 3)
            plog = state["plog"]
            while state["issued"] < min(n + 3, len(plog)):
                i = state["issued"]
                state["issued"] += 1
                psrc, pk = plog[i]
                pt = pans[i % 3]
                P.dma("pool", lambda e, pt=pt, psrc=psrc, pk=pk: e.dma_start(
                    out=pt[:, 0:pk, :], in_=psrc.rearrange("(kc p) n -> p kc n", p=128)),
                    writes=[("pan", i % 3)])
            return pans[n % 3], ("pan", n % 3)

        def mm(out, lhsT, rhs, start, stop, reads, writes):
            P.op("pe", lambda e, out=out, lhsT=lhsT, rhs=rhs, start=start, stop=stop: e.matmul(
                out, lhsT, rhs, start=start, stop=stop), reads=reads, writes=writes)

        def load_const():
            P.dma("pool", lambda e: e.dma_start(out=cst[:, :], in_=cst_d),
                  writes=["cst"])
            P.dma("sp", lambda e: e.dma_start(out=ident_f[:, :], in_=cst_d[:, 0:128]), writes=["ident_f"])
            P.dma("sp", lambda e: e.dma_start(out=gcols[:, :], in_=gcols_d), writes=["gcols"])

        def norm_T(gi, do_norm=True):
            for tt in range(4):
                if do_norm:
                    P.op("act", lambda e, tt=tt: e.activation(
                        out=mb[:, tt, :], in_=h[:, tt, :], func=AF.Square, accum_out=stat[:, tt:tt + 1]),
                        reads=[("h", tt)], writes=[("mb", tt), ("stat", tt)])
                    P.op("dve", lambda e, tt=tt: e.tensor_scalar(
                        out=stat[:, 8 + tt:9 + tt], in0=stat[:, tt:tt + 1], scalar1=1.0 / D, scalar2=EPS,
                        op0=ALU.mult, op1=ALU.add), reads=[("stat", tt)], writes=[("rstd", tt)])
                    P.op("act", lambda e, tt=tt: e.activation(
                        out=stat[:, 8 + tt:9 + tt], in_=stat[:, 8 + tt:9 + tt], func=AF.Sqrt),
                        reads=[("rstd", tt)], writes=[("rstd", tt)])
                    P.op("dve", lambda e, tt=tt: e.reciprocal(
                        out=stat[:, 8 + tt:9 + tt], in_=stat[:, 8 + tt:9 + tt]),
                        reads=[("rstd", tt)], writes=[("rstd", tt)])
                    P.op("dve", lambda e, tt=tt: e.tensor_scalar(
                        out=mb[:, tt, :], in0=h[:, tt, :], scalar1=stat[:, 8 + tt:9 + tt], scalar2=None,
                        op0=ALU.mult), reads=[("h", tt), ("rstd", tt)], writes=[("mb", tt)])
                else:
                    P.op("dve", lambda e, tt=tt: e.tensor_copy(out=mb[:, tt, :], in_=h[:, tt, :]),
                         reads=[("h", tt)], writes=[("mb", tt)])
            for kc in range(16):
                b = nb()
                pb = banks[b][:, :].bitcast(BF16)
                for tt in range(4):
                    P.op("pe", lambda e, pb=pb, tt=tt, kc=kc: e.transpose(
                        pb[:, tt * 128:(tt + 1) * 128], mb[:, tt, kc * 128:(kc + 1) * 128], ident),
                        reads=[("mb", tt), "cst"], writes=[bres(b)])
                if gi is not None:
                    sc = gcols[:, gi * 16 + kc: gi * 16 + kc + 1]
                else:
                    sc = 1.0
                if kc % 2 == 0:
                    P.op("act", lambda e, pb=pb, kc=kc, sc=sc: e.activation(
                        out=xnT[:, kc, :], in_=pb[:, 0:512], func=AF.Copy, scale=sc),
                        reads=[bres(b), "gcols"], writes=[("xnT", kc)])
                else:
                    P.op("dve", lambda e, pb=pb, kc=kc, sc=sc: e.tensor_scalar(
                        out=xnT[:, kc, :], in0=pb[:, 0:512], scalar1=sc, scalar2=None, op0=ALU.mult),
                        reads=[bres(b), "gcols"], writes=[("xnT", kc)])

        def proj_fm(w2d, ncols, evac):
            for cb in range(ncols // 512):
                bs = [nb() for _ in range(4)]
                for kh in range(2):
                    pt, pr = panel(w2d[kh * 1024:(kh + 1) * 1024, cb * 512:(cb + 1) * 512])
                    for m in range(4):
                        for kc in range(8):
                            kg = kh * 8 + kc
                            mm(banks[bs[m]][:, :], pt[:, kc, m * 128:(m + 1) * 128], xnT[:, kg, :],
                               kg == 0, kg == 15, [pr, ("xnT", kg)], [bres(bs[m])])
                for m in range(4):
                    evac(cb * 4 + m, bs[m])

        def proj_tm_group(lhs_of, lhs_res_of, w2d, K, col0):
            bs = [nb() for _ in range(4)]
            nkp = K // 1024
            for kp in range(nkp):
                pt, pr = panel(w2d[kp * 1024:(kp + 1) * 1024, col0:col0 + 512])
                for tt in range(4):
                    for kc in range(8):
                        kg = kp * 8 + kc
                        mm(banks[bs[tt]][:, :], lhs_of(kg, tt), pt[:, kc, :],
                           kg == 0, kg == K // 128 - 1, [pr, lhs_res_of(kg)], [bres(bs[tt])])
            return bs

        def load_grow(gri):
            P.dma("sp", lambda e: e.dma_start(out=gbc[:, :], in_=grows_d[gri:gri + 1, :].partition_broadcast(128)),
                  writes=["gbc"])

        def post_norm_add():
            for tt in range(4):
                P.op("act", lambda e, tt=tt: e.activation(
                    out=xnT[:, 0:4, :], in_=mb[:, tt, :].rearrange("p (a b) -> p a b", a=4), func=AF.Square,
                    accum_out=stat[:, 16 + tt:17 + tt]),
                    reads=[("mb", tt)], writes=[("xnT", 0), ("xnT", 1), ("xnT", 2), ("xnT", 3), ("stat2", tt)])
                P.op("dve", lambda e, tt=tt: e.tensor_scalar(
                    out=stat[:, 24 + tt:25 + tt], in0=stat[:, 16 + tt:17 + tt], scalar1=1.0 / D, scalar2=EPS,
                    op0=ALU.mult, op1=ALU.add), reads=[("stat2", tt)], writes=[("rstd2", tt)])
                P.op("act", lambda e, tt=tt: e.activation(
                    out=stat[:, 24 + tt:25 + tt], in_=stat[:, 24 + tt:25 + tt], func=AF.Sqrt),
                    reads=[("rstd2", tt)], writes=[("rstd2", tt)])
                P.op("dve", lambda e, tt=tt: e.reciprocal(
                    out=stat[:, 24 + tt:25 + tt], in_=stat[:, 24 + tt:25 + tt]),
                    reads=[("rstd2", tt)], writes=[("rstd2", tt)])
                P.op("dve", lambda e, tt=tt: e.scalar_tensor_tensor(
                    out=mb[:, tt, :], in0=mb[:, tt, :], scalar=stat[:, 24 + tt:25 + tt], in1=gbc[:, :],
                    op0=ALU.mult, op1=ALU.mult), reads=[("mb", tt), ("rstd2", tt), "gbc"], writes=[("mb", tt)])
                P.op("pool", lambda e, tt=tt: e.tensor_tensor(
                    out=h[:, tt, :], in0=h[:, tt, :], in1=mb[:, tt, :], op=ALU.add),
                    reads=[("h", tt), ("mb", tt)], writes=[("h", tt)])

        def evac_to_mb(fb, bs):
            for tt in range(4):
                b = bs[tt]
                if tt % 2 == 0:
                    P.op("act", lambda e, b=b, tt=tt, fb=fb: e.activation(
                        out=mb[:, tt, fb * 512:(fb + 1) * 512], in_=banks[b][:, :], func=AF.Copy),
                        reads=[bres(b)], writes=[("mb", tt)])
                else:
                    P.op("dve", lambda e, b=b, tt=tt, fb=fb: e.tensor_copy(
                        out=mb[:, tt, fb * 512:(fb + 1) * 512], in_=banks[b][:, :]),
                        reads=[bres(b)], writes=[("mb", tt)])

        hid = carve(0, [128, 64, TT], BF16)

        def mlp(i):
            norm_T(5 + i)
            P.alias([("hid", m) for m in range(64)], ["R"])

            def ev(m, b):
                s_ = sg[state["sg"]]
                sr = ("sg", state["sg"])
                state["sg"] ^= 1
                P.op("act", lambda e, b=b, s_=s_: e.activation(out=s_[:, :], in_=banks[b][:, :], func=AF.Relu),
                     reads=[bres(b)], writes=[sr])
                P.op("dve", lambda e, m=m, s_=s_: e.tensor_tensor(out=hid[:, m, :], in0=s_[:, :], in1=s_[:, :],
                                                                 op=ALU.mult),
                     reads=[sr], writes=[("hid", m)])
            proj_fm(w1_d[i], DFF, ev)
            load_grow(4 + i)
            for fb in range(4):
                bs = proj_tm_group(lambda kg, tt: hid[:, kg, tt * 128:(tt + 1) * 128], lambda kg: ("hid", kg),
                                   w2_d[i], DFF, fb * 512)
                evac_to_mb(fb, bs)
            post_norm_add()
            P.alias(["R"], [("hid", m) for m in range(64)])

        pTt = carve(0, [128, 2, TT], BF16)

        def ple(i, seq, ti):
            norm_T(None, do_norm=False)
            P.alias(["pTt"], ["R"])
            P.dma("pool", lambda e: e.dma_start(
                out=pTt[:, :, :], in_=pT_d[i, seq, :, ti * TT:(ti + 1) * TT].rearrange("(kc p) t -> p kc t", p=128)),
                writes=["pTt"])
            for fb in range(4):
                gb = proj_tm_group(lambda kg, tt: xnT[:, kg, tt * 128:(tt + 1) * 128], lambda kg: ("xnT", kg),
                                   gate_d[i], D, fb * 512)
                eb = [nb() for _ in range(4)]
                pt, pr = panel(plew_d[i][:, fb * 512:(fb + 1) * 512], nkc=2)
                for tt in range(4):
                    for kc in range(2):
                        mm(banks[eb[tt]][:, :], pTt[:, kc, tt * 128:(tt + 1) * 128], pt[:, kc, :],
                           kc == 0, kc == 1, [pr, "pTt"], [bres(eb[tt])])
                for tt in range(4):
                    s_ = sg[state["sg"]]
                    sr = ("sg", state["sg"])
                    state["sg"] ^= 1
                    P.op("act", lambda e, b=gb[tt], s_=s_: e.activation(out=s_[:, :], in_=banks[b][:, :],
                                                                       func=AF.Sigmoid),
                         reads=[bres(gb[tt])], writes=[sr])
                    P.op("dve", lambda e, b=eb[tt], s_=s_: e.tensor_tensor(out=s_[:, :], in0=banks[b][:, :],
                                                                          in1=s_[:, :], op=ALU.mult),
                         reads=[bres(eb[tt]), sr], writes=[sr])
                    P.op("pool", lambda e, tt=tt, fb=fb, s_=s_: e.tensor_tensor(
                        out=h[:, tt, fb * 512:(fb + 1) * 512], in0=h[:, tt, fb * 512:(fb + 1) * 512],
                        in1=s_[:, :], op=ALU.add), reads=[("h", tt), sr], writes=[("h", tt)])
            P.alias(["R"], ["pTt"])

        def cmul(eng, outr, outi, ar, ai, br, bi, t1, t2, res_in, res_out, tag="", xw=()):
            TTm = ALU.mult
            T1, T2 = "t1" + tag, "t2" + tag
            P.op(eng, lambda e: e.tensor_tensor(out=t1, in0=ar, in1=br, op=TTm), reads=res_in, writes=[T1] + list(xw))
            P.op(eng, lambda e: e.tensor_tensor(out=t2, in0=ai, in1=bi, op=TTm), reads=res_in, writes=[T2] + list(xw))
            P.op(eng, lambda e: e.tensor_tensor(out=outr, in0=t1, in1=t2, op=ALU.subtract), reads=[T1, T2],
                 writes=res_out)
            P.op(eng, lambda e: e.tensor_tensor(out=t1, in0=ar, in1=bi, op=TTm), reads=res_in + res_out,
                 writes=[T1] + list(xw))
            P.op(eng, lambda e: e.tensor_tensor(out=t2, in0=ai, in1=br, op=TTm), reads=res_in + res_out,
                 writes=[T2] + list(xw))
            P.op(eng, lambda e: e.tensor_tensor(out=outi, in0=t1, in1=t2, op=ALU.add), reads=[T1, T2],
                 writes=res_out)

        def lam_calc(ar, ai, ls, lr, li, fr, fi, tA, tB, tC, tD, tE):
            V = "dve"
            rs = ["lamin", "lamout"]
            ws = ["lamout"]

            def o(fn):
                P.op(V, fn, reads=rs, writes=ws)

            def horner(q, z, coefs):
                o(lambda e: e.tensor_scalar(out=q, in0=z, scalar1=coefs[-1], scalar2=None, op0=ALU.mult))
                for c in reversed(coefs[:-1]):
                    o(lambda e, c=c: e.scalar_tensor_tensor(out=q, in0=q, scalar=c, in1=z, op0=ALU.add, op1=ALU.mult))
            fact = [1.0]
            for k in range(1, 20):
                fact.append(fact[-1] * k)
            o(lambda e: e.tensor_scalar(out=tB, in0=ls, scalar1=0.125, scalar2=None, op0=ALU.mult))
            horner(tA, tB, [1.0 / fact[k] for k in range(1, 13)])
            o(lambda e: e.tensor_scalar(out=tA, in0=tA, scalar1=1.0, scalar2=None, op0=ALU.add))
            for _ in range(3):
                o(lambda e: e.tensor_tensor(out=tA, in0=tA, in1=tA, op=ALU.mult))
            o(lambda e: e.tensor_tensor(out=tC, in0=ar, in1=tA, op=ALU.mult))
            horner(tB, tC, [1.0 / fact[k] for k in range(1, 9)])
            o(lambda e: e.tensor_scalar(out=tB, in0=tB, scalar1=1.0, scalar2=None, op0=ALU.add))
            o(lambda e: e.tensor_tensor(out=tA, in0=ai, in1=tA, op=ALU.mult))
            MAGIC = 12582912.0
            o(lambda e: e.tensor_scalar(out=tC, in0=tA, scalar1=1.0 / (2 * math.pi), scalar2=None, op0=ALU.mult))
            o(lambda e: e.tensor_scalar(out=tC, in0=tC, scalar1=MAGIC, scalar2=None, op0=ALU.add))
            o(lambda e: e.tensor_scalar(out=tC, in0=tC, scalar1=-MAGIC, scalar2=None, op0=ALU.add))
            C1 = 6.28125
            C2 = 2 * math.pi - C1
            o(lambda e: e.scalar_tensor_tensor(out=tA, in0=tC, scalar=-C1, in1=tA, op0=ALU.mult, op1=ALU.add))
            o(lambda e: e.scalar_tensor_tensor(out=tA, in0=tC, scalar=-C2, in1=tA, op0=ALU.mult, op1=ALU.add))
            o(lambda e: e.tensor_scalar(out=tA, in0=tA, scalar1=0.5, scalar2=None, op0=ALU.mult))
            o(lambda e: e.tensor_tensor(out=tC, in0=tA, in1=tA, op=ALU.mult))
            horner(tD, tC, [(-1.0) ** k / fact[2 * k + 1] for k in range(1, 8)])
            o(lambda e: e.scalar_tensor_tensor(out=tD, in0=tD, scalar=1.0, in1=tA, op0=ALU.add, op1=ALU.mult))
            horner(tE, tC, [(-1.0) ** k / fact[2 * k] for k in range(1, 9)])
            o(lambda e: e.tensor_scalar(out=tE, in0=tE, scalar1=1.0, scalar2=None, op0=ALU.add))
            o(lambda e: e.tensor_tensor(out=li, in0=tD, in1=tE, op=ALU.mult))
            o(lambda e: e.scalar_tensor_tensor(out=li, in0=li, scalar=2.0, in1=tB, op0=ALU.mult, op1=ALU.mult))
            o(lambda e: e.tensor_tensor(out=lr, in0=tD, in1=tD, op=ALU.mult))
            o(lambda e: e.tensor_scalar(out=lr, in0=lr, scalar1=-2.0, scalar2=1.0, op0=ALU.mult, op1=ALU.add))
            o(lambda e: e.tensor_tensor(out=lr, in0=lr, in1=tB, op=ALU.mult))
            o(lambda e: e.tensor_tensor(out=tA, in0=ar, in1=ar, op=ALU.mult))
            o(lambda e: e.tensor_tensor(out=tB, in0=ai, in1=ai, op=ALU.mult))
            o(lambda e: e.tensor_tensor(out=tA, in0=tA, in1=tB, op=ALU.add))
            o(lambda e: e.reciprocal(out=tA, in_=tA))
            o(lambda e: e.tensor_scalar(out=tB, in0=lr, scalar1=-1.0, scalar2=None, op0=ALU.add))
            o(lambda e: e.tensor_tensor(out=fr, in0=tB, in1=ar, op=ALU.mult))
            o(lambda e: e.tensor_tensor(out=tC, in0=li, in1=ai, op=ALU.mult))
            o(lambda e: e.tensor_tensor(out=fr, in0=fr, in1=tC, op=ALU.add))
            o(lambda e: e.tensor_tensor(out=fr, in0=fr, in1=tA, op=ALU.mult))
            o(lambda e: e.tensor_tensor(out=fi, in0=li, in1=ar, op=ALU.mult))
            o(lambda e: e.tensor_tensor(out=tC, in0=tB, in1=ai, op=ALU.mult))
            o(lambda e: e.tensor_tensor(out=fi, in0=fi, in1=tC, op=ALU.subtract))
            o(lambda e: e.tensor_tensor(out=fi, in0=fi, in1=tA, op=ALU.mult))

        def s5_prep(l):
            P.alias(["prep", "lamin", "lamout", "t1", "t2", "t1p", "t2p", "G", "Bt", "Vt", "Ot", "Xt", "Ct", "BDt",
                     "wt", "Cb", "dm", "stg", ("Gk", 0), ("Gk", 1)], ["R"])
            KB = 1024
            o = 0
            Gs = carve(o, [128, 2, 8, 64], F32); o += 4096
            Hs = carve(o, [128, 2, 8, 64], F32); o += 4096
            base = o
            a3 = carve(o, [128, 3, 64], F32); o += 768
            sm = carve(o, [128, 9, 64], F32); o += 2304
            tt1 = carve(o, [128, 64], F32); o += 256
            tt2 = carve(o, [128, 64], F32); o += 256
            stg = carve(o, [128, 512], F32); o += 2048
            P.dma("sp", lambda e: e.dma_start(out=a3[:, :, :], in_=a3_d[l].rearrange("k p n -> p k n")),
                  writes=["lamin"])
            lam_calc(a3[:, 0, :], a3[:, 1, :], a3[:, 2, :], sm[:, 0, :], sm[:, 1, :], sm[:, 2, :], sm[:, 3, :],
                     sm[:, 4, :], sm[:, 5, :], sm[:, 6, :], sm[:, 7, :], sm[:, 8, :])
            P.op("dve", lambda e: e.tensor_copy(out=Gs[:, 0, 0, :], in_=sm[:, 2, :]), reads=["lamout"], writes=["G"])
            P.op("dve", lambda e: e.tensor_copy(out=Gs[:, 1, 0, :], in_=sm[:, 3, :]), reads=["lamout"], writes=["G"])
            P.op("dve", lambda e: e.tensor_copy(out=Hs[:, 0, 0, :], in_=sm[:, 0, :]), reads=["lamout"], writes=["G"])
            P.op("dve", lambda e: e.tensor_copy(out=Hs[:, 1, 0, :], in_=sm[:, 1, :]), reads=["lamout"], writes=["G"])
            for k in range(7):
                cmul("dve", Gs[:, 0, k + 1, :], Gs[:, 1, k + 1, :], sm[:, 0, :], sm[:, 1, :], Gs[:, 0, k, :],
                     Gs[:, 1, k, :], tt1, tt2, ["lamout", "G"], ["G"])
                cmul("dve", Hs[:, 0, k + 1, :], Hs[:, 1, k + 1, :], sm[:, 0, :], sm[:, 1, :], Hs[:, 0, k, :],
                     Hs[:, 1, k, :], tt1, tt2, ["lamout", "G"], ["G"])
            P.op("dve", lambda e: e.tensor_copy(out=A8[:, l, 0, :], in_=Hs[:, 0, 7, :]), reads=["G"], writes=["A8"])
            P.op("dve", lambda e: e.tensor_copy(out=A8[:, l, 1, :], in_=Hs[:, 1, 7, :]), reads=["G"], writes=["A8"])
            Gs2 = Gs.rearrange("p a b c -> p (a b c)")
            for half in range(2):
                b = nb()
                for q in range(4):
                    blk = half * 4 + q
                    P.op("pe", lambda e, b=b, q=q, blk=blk: e.transpose(
                        banks[b][:, q * 128:(q + 1) * 128], Gs2[:, blk * 128:(blk + 1) * 128], ident_f[:, :]),
                        reads=["G", "ident_f"], writes=[bres(b)])
                P.op("dve", lambda e, b=b: e.tensor_copy(out=stg[:, :], in_=banks[b][:, :]),
                     reads=[bres(b)], writes=["stg"])
                P.dma("sp", lambda e, half=half: e.dma_start(
                    out=Gsd[l, half * 512:(half + 1) * 512, :].rearrange("(q p) c -> p q c", p=128),
                    in_=stg.rearrange("p (q c) -> p q c", q=4)), reads=["stg"], writes=["Gsd"])
            PJ = ["Bt", ("Ct", 0), ("Ct", 1), "Cb", "Vt", ("Xt", 0), ("Xt", 1), "BDt", "wt", "dm", ("Gk", 0), ("Gk", 1),
                  "t1", "t2"]
            P.alias(PJ, ["lamin", "lamout", "stg", "t1", "t2"])
            HN = ["Ot", "wte", "t1p", "t2p"]
            P.alias(HN, HALL)
            hb = h.rearrange("p a b -> p (a b)")
            Ot = hb[:, 0:4096].bitcast(BF16).rearrange("p (a b c d) -> p a b c d", a=4, b=8, c=2)
            w2 = hb[:, 4096:5120].rearrange("p (k q c) -> p k q c", k=2, q=4)
            u1 = hb[:, 5120:5632].rearrange("p (q c) -> p q c", q=4)
            u2 = hb[:, 5632:6144].rearrange("p (q c) -> p q c", q=4)
            o = base
            Ct2 = [carve(o + i * 4 * KB, [128, 2, 4, 128], F32) for i in range(2)]; o += 8 * KB
            Cb = carve(o, [128, 2, 4, 128], BF16); o += 2 * KB
            BDt = carve(o, [128, 8, 128], BF16); o += 2 * KB
            dm = carve(o, [128, 128], F32); o += KB // 2
            Xt = [carve(o + i * 2 * KB, [128, 2, 4, 128], BF16) for i in range(2)]; o += 4 * KB
            Bt = carve(o, [128, 2, 512], F32); o += 4 * KB
            Gk = [carve(o + i * 4 * KB, [128, 2, 512], F32) for i in range(2)]; o += 8 * KB
            tq1 = carve(o, [128, 512], F32); o += 2 * KB
            tq2 = carve(o, [128, 512], F32); o += 2 * KB
            Vt = carve(o, [128, 4, 8, 2, 128], BF16); o += 16 * KB
            wt = carve(o, [128, 2, 512], F32); o += 4 * KB
            for j in range(16):
                ct_ = Ct2[j % 2]
                cr_ = ("Ct", j % 2)
                P.dma("sp", lambda e, j=j: e.dma_start(
                    out=Bt[:, :, :], in_=bL1_d[l, :, j, :, :].rearrange("k p n -> p k n")), writes=["Bt"])
                P.dma("sp", lambda e, ct_=ct_, j=j: e.dma_start(
                    out=ct_.rearrange("p k q c -> p k (q c)"), in_=cL2_d[l, :, j, :, :].rearrange("k p n -> p k n")),
                    writes=[cr_])
                P.dma("sp", lambda e, j=j: e.dma_start(out=dm[:, :], in_=dmat_d[l, j]), writes=["dm"])
                P.op("dve", lambda e, ct_=ct_: e.tensor_copy(out=Cb[:, 0, :, :], in_=ct_[:, 0, :, :]),
                     reads=[cr_], writes=["Cb"])
                P.op("dve", lambda e, ct_=ct_: e.tensor_scalar(out=Cb[:, 1, :, :], in0=ct_[:, 1, :, :],
                                                               scalar1=-1.0, scalar2=None, op0=ALU.mult),
                     reads=[cr_], writes=["Cb"])
                for k in range(8):
                    s_ = 7 - k
                    gb_ = Gk[k % 2]
                    gr_ = ("Gk", k % 2)
                    xt_ = Xt[k % 2]
                    xr_ = ("Xt", k % 2)
                    P.dma("sp", lambda e, gb_=gb_, j=j, k=k: e.dma_start(
                        out=gb_[:, :, :],
                        in_=Gsd[l].rearrange("(r m) c -> r (m c)", r=2)[:, (k * 64 + 4 * j) * 128:(k * 64 + 4 * j + 4) * 128]
                        .partition_broadcast(128)), reads=["Gsd"], writes=[gr_])
                    cmul("dve", wt[:, 0, :], wt[:, 1, :], gb_[:, 0, :], gb_[:, 1, :], Bt[:, 0, :],
                         Bt[:, 1, :], tq1, tq2, [gr_, "Bt"], ["wt"])
                    for ri in range(2):
                        P.op("act", lambda e, s_=s_, ri=ri: e.activation(
                            out=Vt[:, :, s_, ri, :], in_=wt[:, ri, :].rearrange("p (q m) -> p q m", q=4),
                            func=AF.Copy), reads=["wt"], writes=["Vt"])
                    tbk = nb()
                    pb = banks[tbk][:, :].bitcast(BF16)
                    for ri in range(2):
                        for q in range(4):
                            P.op("pe", lambda e, pb=pb, ri=ri, q=q, s_=s_: e.transpose(
                                pb[:, (ri * 4 + q) * 128:(ri * 4 + q + 1) * 128], Vt[:, q, s_, ri, :], ident),
                                reads=["Vt", "cst"], writes=[bres(tbk)])
                    P.op("act", lambda e, pb=pb, xt_=xt_: e.activation(
                        out=xt_.rearrange("p r q c -> p (r q c)"), in_=pb[:, :], func=AF.Copy),
                        reads=[bres(tbk)], writes=[xr_])
                    if k % 4 == 0:
                        b = nb()
                    kk = k % 4
                    n = 0
                    for q in range(4):
                        for ri in range(2):
                            mm(banks[b][:, kk * 128:(kk + 1) * 128], xt_[:, ri, q, :], Cb[:, ri, q, :],
                               (kk == 0 and n == 0), (kk == 3 and n == 7), [xr_, "Cb"], [bres(b)])
                            n += 1
                    if kk == 3:
                        k4 = k // 4
                        if k4 == 0:
                            P.op("dve", lambda e, b=b: e.tensor_tensor(
                                out=banks[b][:, 0:128], in0=banks[b][:, 0:128], in1=dm[:, :], op=ALU.add),
                                reads=[bres(b), "dm"], writes=[bres(b)])
                        P.op("act", lambda e, b=b, k4=k4: e.activation(
                            out=BDt[:, 4 * k4:4 * k4 + 4, :], in_=banks[b][:, :].rearrange("p (k c) -> p k c", k=4),
                            func=AF.Copy), reads=[bres(b)], writes=["BDt"])
                P.dma("sp", lambda e, j=j: e.dma_start(
                    out=Vd[l, j], in_=Vt.rearrange("p a b c d -> p (a b c d)")), reads=["Vt"], writes=["Vd"])
                P.dma("sp", lambda e, j=j: e.dma_start(
                    out=BDd[l, j], in_=BDt.rearrange("p a b -> p (a b)")), reads=["BDt"], writes=["BDd"])

                def bc(small, ri, k, j=j):
                    return small[:, ri, k, 4 * j:4 * j + 4].unsqueeze(2).to_broadcast([128, 4, 128])
                for k in range(8):
                    cmul("pool", w2[:, 0, :, :], w2[:, 1, :, :], bc(Hs, 0, k), bc(Hs, 1, k), ct_[:, 0, :, :],
                         ct_[:, 1, :, :], u1, u2, ["G", cr_], ["wte"], tag="p")
                    P.op("act", lambda e, k=k: e.activation(out=Ot[:, :, k, 0, :], in_=w2[:, 0, :, :], func=AF.Copy),
                         reads=["wte"], writes=["Ot"])
                    P.op("act", lambda e, k=k: e.activation(out=Ot[:, :, k, 1, :], in_=w2[:, 1, :, :], func=AF.Copy,
                                                            scale=-1.0),
                         reads=["wte"], writes=["Ot"])
                P.dma("sp", lambda e, j=j: e.dma_start(
                    out=Od[l, j], in_=Ot.rearrange("p a b c d -> p (a b c d)")), reads=["Ot"], writes=["Od"])
            P.alias(HALL, HN)
            P.alias(["R"], ["prep", "lamin", "lamout", "G", "stg", "wtx", "t1p", "t2p"] + PJ)

        def s5(l, first_tile):
            KB = 1024
            o = 0
            uT = carve(o, [128, 16, TT], BF16); o += 16 * KB
            VO = [carve(o + k * 4 * KB, [128, 8, 2, 128], BF16) for k in range(4)]; o += 16 * KB
            BDb = [carve(o + k * 2 * KB, [128, 8, 128], BF16) for k in range(2)]; o += 4 * KB
            Sb0 = carve(o, [128, NCH, 2, 32], F32); o += 16 * KB
            Xp0 = carve(o, [128, 2, 32, NCH], BF16); o += 8 * KB
            sc = carve(o, [128, 2, 2, 32], F32); o += KB
            Sb1 = xnT.rearrange("p a b -> p (a b)").bitcast(F32).rearrange("p (c r q) -> p c r q", r=2, q=32)
            Xp1 = mb[:, 0:2, :].rearrange("p a b -> p (a b)").rearrange("p (r q c) -> p r q c", r=2, q=32)
            Sbs = [Sb0, Sb1]
            Xps = [Xp0, Xp1]
            sbn = [[("Sbc", hf, c) for c in range(NCH)] for hf in range(2)]
            names = ([("fa", j) for j in range(16)] + [("VO", k) for k in range(4)]
                     + [("BDb", k) for k in range(2)] + sbn[0] + [("Xp", 0), "st", "su0", "su1"])
            xn_all = [("xnT", kc) for kc in range(16)]
            norm_T(l)
            P.alias(names, ["R"])

            def ev_u(m, b):
                dst = uT[:, m, :].rearrange("p (s c) -> p c s", s=T8)
                src = banks[b][:, :].rearrange("p (c s) -> p c s", s=T8)
                if m % 2 == 0:
                    P.op("act", lambda e, dst=dst, src=src: e.activation(out=dst, in_=src, func=AF.Copy),
                         reads=[bres(b)], writes=[("fa", m)])
                else:
                    P.op("dve", lambda e, dst=dst, src=src: e.tensor_copy(out=dst, in_=src),
                         reads=[bres(b)], writes=[("fa", m)])
            proj_fm(w_in_d[l], D, ev_u)
            P.alias(sbn[1], xn_all)
            P.alias([("Xp", 1)], [("mb", 0), ("mb", 1)])
            if first_tile:
                P.op("dve", lambda e: e.memset(Xst[:, l, :, :], 0.0), writes=[("Xst", l)])
            cnt = {"vo": 0, "bd": 0}
            for half in range(2):
                Sb = Sbs[half]
                for jj in range(8):
                    j = half * 8 + jj
                    b = nb()
                    for q in range(4):
                        k = cnt["vo"] % 4
                        cnt["vo"] += 1
                        P.dma("sp", lambda e, k=k, j=j, q=q: e.dma_start(
                            out=VO[k].rearrange("p a b c -> p (a b c)"), in_=Vd[l, j][:, q * 2048:(q + 1) * 2048]),
                            reads=["Vd"], writes=[("VO", k)])
                        for ri in range(2):
                            for s in range(T8):
                                mm(banks[b][:, (q * 2 + ri) * NCH:(q * 2 + ri + 1) * NCH], VO[k][:, s, ri, :],
                                   uT[:, j, s * NCH:(s + 1) * NCH], (q == 0 and ri == 0 and s == 0),
                                   (q == 3 and ri == 1 and s == T8 - 1),
                                   [("VO", k), ("fa", j)], [bres(b)])
                    P.op("act", lambda e, b=b, jj=jj, Sb=Sb: e.activation(
                        out=Sb[:, :, :, 4 * jj:4 * jj + 4].rearrange("p c r q -> p q r c"),
                        in_=banks[b][:, :].rearrange("p (q r c) -> p q r c", q=4, r=2), func=AF.Copy),
                        reads=[bres(b)], writes=sbn[half])
            for half in range(2):
                Sb = Sbs[half]
                Xp = Xps[half]
                Ar2 = A8[:, l, 0, 32 * half:32 * half + 32].unsqueeze(1).to_broadcast([128, 2, 32])
                Ai = A8[:, l, 1, 32 * half:32 * half + 32]
                for c in range(NCH):
                    if c == 0:
                        Xprev = Xst[:, l, :, 32 * half:32 * half + 32]
                        pres = ("Xst", l)
                    else:
                        Xprev = Sb[:, c - 1, :, :]
                        pres = ("Sbc", half, c - 1)
                    P.op("dve", lambda e, Xprev=Xprev, Ar2=Ar2: e.tensor_tensor(out=sc[:, 0, :, :], in0=Xprev, in1=Ar2,
                                                                                op=ALU.mult),
                         reads=[pres, "A8"], writes=["st"])
                    P.op("dve", lambda e, Xprev=Xprev, Ai=Ai: e.scalar_tensor_tensor(
                        out=sc[:, 1, 0, :], in0=Xprev[:, 1, :], scalar=-1.0, in1=Ai, op0=ALU.mult, op1=ALU.mult),
                        reads=[pres, "A8"], writes=["su0"])
                    P.op("dve", lambda e, Xprev=Xprev, Ai=Ai: e.tensor_tensor(out=sc[:, 1, 1, :], in0=Xprev[:, 0, :],
                                                                              in1=Ai, op=ALU.mult),
                         reads=[pres, "A8"], writes=["su1"])
                    P.op("dve", lambda e, c=c, Sb=Sb: e.tensor_tensor(out=Sb[:, c, :, :], in0=Sb[:, c, :, :],
                                                                      in1=sc[:, 0, :, :], op=ALU.add),
                         reads=["st", ("Sbc", half, c)], writes=[("Sbc", half, c)])
                    P.op("dve", lambda e, c=c, Sb=Sb: e.tensor_tensor(out=Sb[:, c, :, :], in0=Sb[:, c, :, :],
                                                                      in1=sc[:, 1, :, :], op=ALU.add),
                         reads=["su0", "su1", ("Sbc", half, c)], writes=[("Sbc", half, c)])
                P.op("act", lambda e, half=half, Xp=Xp: e.activation(out=Xp[:, :, :, 0],
                                                                     in_=Xst[:, l, :, 32 * half:32 * half + 32],
                                                                     func=AF.Copy), reads=[("Xst", l)], writes=[("Xp", half)])
                P.op("act", lambda e, Xp=Xp, Sb=Sb: e.activation(out=Xp[:, :, :, 1:NCH],
                                                                 in_=Sb[:, 0:NCH - 1, :, :].rearrange("p c r q -> p r q c"),
                                                                 func=AF.Copy), reads=sbn[half], writes=[("Xp", half)])
                P.op("dve", lambda e, half=half, Sb=Sb: e.tensor_copy(out=Xst[:, l, :, 32 * half:32 * half + 32],
                                                                      in_=Sb[:, NCH - 1, :, :]),
                     reads=sbn[half] + [("Xp", half)], writes=[("Xst", l)])
            for half in range(2):
                Xp = Xps[half]
                for jj in range(8):
                    j = half * 8 + jj
                    kb = cnt["bd"] % 2
                    cnt["bd"] += 1
                    P.dma("sp", lambda e, kb=kb, j=j: e.dma_start(
                        out=BDb[kb].rearrange("p a b -> p (a b)"), in_=BDd[l, j]),
                        reads=["BDd"], writes=[("BDb", kb)])
                    b = nb()
                    for t in range(T8):
                        for s in range(t + 1):
                            mm(banks[b][:, t * NCH:(t + 1) * NCH], BDb[kb][:, t - s, :], uT[:, j, s * NCH:(s + 1) * NCH],
                               (t == 0 and s == 0), False, [("BDb", kb), ("fa", j)], [bres(b)])
                    for q in range(4):
                        k = cnt["vo"] % 4
                        cnt["vo"] += 1
                        P.dma("sp", lambda e, k=k, j=j, q=q: e.dma_start(
                            out=VO[k].rearrange("p a b c -> p (a b c)"), in_=Od[l, j][:, q * 2048:(q + 1) * 2048]),
                            reads=["Od"], writes=[("VO", k)])
                        for t in range(T8):
                            for ri in range(2):
                                mm(banks[b][:, t * NCH:(t + 1) * NCH], VO[k][:, t, ri, :],
                                   Xp[:, ri, 4 * jj + q, :], False, (q == 3 and ri == 1 and t == T8 - 1),
                                   [("VO", k), ("Xp", half)], [bres(b)])
                    s_ = sg[state["sg"]]
                    sr = ("sg", state["sg"])
                    state["sg"] ^= 1
                    s2 = sg[state["sg"]]
                    sr2 = ("sg", state["sg"])
                    state["sg"] ^= 1
                    P.op("act", lambda e, b=b, s_=s_: e.activation(out=s_[:, :], in_=banks[b][:, :], func=AF.Square),
                         reads=[bres(b)], writes=[sr])
                    P.op("dve", lambda e, s_=s_: e.tensor_scalar(out=s_[:, :], in0=s_[:, :], scalar1=0.044715,
                                                                 scalar2=1.0, op0=ALU.mult, op1=ALU.add),
                         reads=[sr], writes=[sr])
                    P.op("dve", lambda e, b=b, s_=s_: e.tensor_tensor(out=s_[:, :], in0=s_[:, :], in1=banks[b][:, :],
                                                                      op=ALU.mult), reads=[sr, bres(b)], writes=[sr])
                    P.op("act", lambda e, s_=s_, s2=s2: e.activation(out=s2[:, :], in_=s_[:, :], func=AF.Sigmoid,
                                                                     scale=1.5957691216057308),
                         reads=[sr], writes=[sr2])
                    P.op("dve", lambda e, b=b, s2=s2, j=j: e.tensor_tensor(
                        out=uT[:, j, :].rearrange("p (c t) -> p t c", t=T8),
                        in0=s2[:, :].rearrange("p (t c) -> p t c", t=T8),
                        in1=banks[b][:, :].rearrange("p (t c) -> p t c", t=T8), op=ALU.mult),
                        reads=[sr2, bres(b)], writes=[("fa", j)])
            P.alias(xn_all, sbn[1])
            P.alias([("mb", 0), ("mb", 1)], [("Xp", 1)])
            load_grow(l)
            for fb in range(4):
                vb_ = proj_tm_group(lambda kg, tt: uT[:, kg, tt * 128:(tt + 1) * 128], lambda kg: ("fa", kg),
                                    w_glu_d[l], D, fb * 512)
                gb_ = proj_tm_group(lambda kg, tt: uT[:, kg, tt * 128:(tt + 1) * 128], lambda kg: ("fa", kg),
                                    w_glu_d[l], D, D + fb * 512)
                for tt in range(4):
                    s_ = sg[state["sg"]]
                    sr = ("sg", state["sg"])
                    state["sg"] ^= 1
                    P.op("act", lambda e, b=gb_[tt], s_=s_: e.activation(out=s_[:, :], in_=banks[b][:, :],
                                                                        func=AF.Sigmoid),
                         reads=[bres(gb_[tt])], writes=[sr])
                    P.op("dve", lambda e, b=vb_[tt], s_=s_, tt=tt, fb=fb: e.tensor_tensor(
                        out=mb[:, tt, fb * 512:(fb + 1) * 512], in0=banks[b][:, :], in1=s_[:, :], op=ALU.mult),
                        reads=[bres(vb_[tt]), sr], writes=[("mb", tt)])
            post_norm_add()
            P.alias(["R"], names)

        def kv_proj(ti):
            norm_T(2)

            def ev_k(m, b):
                P.op("act", lambda e, b=b, m=m: e.activation(out=kTc[:, m, ti * TT:(ti + 1) * TT], in_=banks[b][:, :],
                                                            func=AF.Copy), reads=[bres(b)], writes=["kTc"])
            proj_fm(w_k_d, 512, ev_k)
            bs = proj_tm_group(lambda kg, tt: xnT[:, kg, tt * 128:(tt + 1) * 128], lambda kg: ("xnT", kg),
                               w_v_d, D, 0)
            for tt in range(4):
                P.op("dve", lambda e, b=bs[tt], tt=tt: e.tensor_copy(out=vvc[:, ti * 4 + tt, :], in_=banks[b][:, :]),
                     reads=[bres(bs[tt])], writes=["vvc"])

        def attn(jb, ti):
            KB = 1024
            o = 0
            qT = carve(o, [128, 16, TT], BF16); o += 16 * KB
            eb = [carve(o + k * 2 * KB, [128, 512], F32) for k in range(4)]; o += 8 * KB
            Lp = [carve(o + k * KB, [128, 512], BF16) for k in range(8)]; o += 8 * KB
            tb = [carve(o + k * 2 * KB, [128, 512], F32) for k in range(8)]; o += 16 * KB
            wT = [carve(o + k * KB, [128, 512], BF16) for k in range(6)]; o += 6 * KB
            names = ([("fa", j) for j in range(16)] + [("eb", k) for k in range(4)] + [("Lp", k) for k in range(8)]
                     + [("tb", k) for k in range(8)] + [("wT", k) for k in range(6)])
            norm_T(3 + jb)
            P.alias(names, ["R"])

            def ev_q(m, b):
                if m % 2 == 0:
                    P.op("act", lambda e, b=b, m=m: e.activation(out=qT[:, m, :], in_=banks[b][:, :], func=AF.Copy),
                         reads=[bres(b)], writes=[("fa", m)])
                else:
                    P.op("dve", lambda e, b=b, m=m: e.tensor_copy(out=qT[:, m, :], in_=banks[b][:, :]),
                         reads=[bres(b)], writes=[("fa", m)])
            proj_fm(w_q_d[jb], D, ev_q)
            streams = [[], []]
            for qb in range(4):
                nkb = ti * 4 + qb + 1
                for kvh in range(4):
                    st_ = kvh % 2
                    for idx, kb in enumerate(range(nkb - 1, -1, -1)):
                        streams[st_].append(dict(qb=qb, kvh=kvh, kb=kb, first=(idx == 0), last=(kb == 0),
                                                 diag=(kb == nkb - 1), acc=st_, ob=2 + st_, s=st_,
                                                 i=len(streams[st_])))

            def qview(u):
                return qT[:, 4 * u["kvh"]:4 * u["kvh"] + 4, u["qb"] * 128:(u["qb"] + 1) * 128]

            def hres_of(u):
                return [("fa", 4 * u["kvh"] + g) for g in range(4)]

            def stageA(u):
                i = u["i"]
                zb = 4 + 2 * u["s"] + i % 2
                e2, l4 = 2 * u["s"] + i % 2, 4 * u["s"] + i % 4
                kb, kvh = u["kb"], u["kvh"]
                mm(banks[zb][:, :].rearrange("p (g q) -> p g q", g=4),
                   kTc[:, kvh, kb * 128:(kb + 1) * 128], qview(u), True, True, ["kTc"] + hres_of(u), [bres(zb)])
                P.op("act", lambda e, zb=zb, e2=e2: e.activation(out=eb[e2][:, :], in_=banks[zb][:, :],
                                                                 func=AF.Exp, scale=SCALE),
                     reads=[bres(zb)], writes=[("eb", e2)])
                P.op("act", lambda e, e2=e2, l4=l4: e.activation(out=Lp[l4][:, :], in_=eb[e2][:, :], func=AF.Ln,
                                                                 bias=1.0),
                     reads=[("eb", e2)], writes=[("Lp", l4)])
                if u["diag"]:
                    P.op("dve", lambda e, l4=l4: e.tensor_tensor(out=Lp[l4][:, :], in0=Lp[l4][:, :],
                                                                 in1=maskd, op=ALU.mult),
                         reads=[("Lp", l4), "cst"], writes=[("Lp", l4)])
                P.op("dve", lambda e, zb=zb, l4=l4: e.scalar_tensor_tensor(
                    out=tb[l4][:, :], in0=banks[zb][:, :], scalar=SCALE, in1=Lp[l4][:, :],
                    op0=ALU.mult, op1=ALU.subtract), reads=[bres(zb), ("Lp", l4)], writes=[("tb", l4)])

            def stageB(u):
                i = u["i"]
                l4, w2_ = 4 * u["s"] + i % 4, 3 * u["s"] + i % 3
                acc = u["acc"]
                mm(banks[acc][:, :], Umat, Lp[l4][:, :], u["first"], False, ["cst", ("Lp", l4)], [bres(acc)])
                P.op("dve", lambda e, acc=acc, l4=l4: e.tensor_tensor(
                    out=tb[l4][:, :], in0=tb[l4][:, :], in1=banks[acc][:, :], op=ALU.subtract),
                    reads=[("tb", l4), bres(acc)], writes=[("tb", l4)])
                if not u["last"]:
                    mm(banks[acc][:, :], Lomat, Lp[l4][:, :], False, False, ["cst", ("Lp", l4)], [bres(acc)])
                P.op("act", lambda e, l4=l4, w2_=w2_: e.activation(out=wT[w2_][:, :], in_=tb[l4][:, :], func=AF.Exp),
                     reads=[("tb", l4)], writes=[("wT", w2_)])
                if u["diag"]:
                    P.op("dve", lambda e, w2_=w2_: e.tensor_tensor(out=wT[w2_][:, :], in0=wT[w2_][:, :],
                                                                   in1=maskd, op=ALU.mult),
                         reads=[("wT", w2_), "cst"], writes=[("wT", w2_)])

            def stageC(u):
                i = u["i"]
                w2_ = 3 * u["s"] + i % 3
                ob = u["ob"]
                kb, kvh = u["kb"], u["kvh"]
                mm(banks[ob][:, :], vvc[:, kb, kvh * 128:(kvh + 1) * 128], wT[w2_][:, :], u["first"], u["last"],
                   ["vvc", ("wT", w2_)], [bres(ob)])
                if u["last"]:
                    qv = qview(u)
                    P.op("act", lambda e, ob=ob, qv=qv: e.activation(
                        out=qv, in_=banks[ob][:, :].rearrange("p (g q) -> p g q", g=4), func=AF.Copy),
                        reads=[bres(ob)], writes=hres_of(u))
            nu = len(streams[0])
            for i in range(min(2, nu)):
                for st_ in range(2):
                    stageA(streams[st_][i])
            for i in range(nu):
                for st_ in range(2):
                    stageB(streams[st_][i])
                if i + 2 < nu:
                    for st_ in range(2):
                        stageA(streams[st_][i + 2])
                if i >= 1:
                    for st_ in range(2):
                        stageC(streams[st_][i - 1])
            for st_ in range(2):
                stageC(streams[st_][nu - 1])
            load_grow(2 + jb)
            for fb in range(4):
                bs = proj_tm_group(lambda kg, tt: qT[:, kg, tt * 128:(tt + 1) * 128], lambda kg: ("fa", kg),
                                   w_o_d[jb], D, fb * 512)
                evac_to_mb(fb, bs)
            post_norm_add()
            P.alias(["R"], names)

        def emit_all():
            load_const()
            P.op("dve", lambda e: e.memset(stat[:, :], 0.0), writes=["statinit"])
            for l in layers_a:
                s5_prep(l)
            for seq in range(NSEQ):
                for ti in range(NT):
                    P.dma("sp", lambda e, seq=seq, ti=ti: e.dma_start(
                        out=h[:, :, :], in_=x_d[seq, ti * TT:(ti + 1) * TT, :].rearrange("(tt p) d -> p tt d", p=128)),
                        writes=HALL)
                    for (kind, i) in plan:
                        if kind == "mix":
                            if i < 2:
                                s5(i, ti == 0)
                            else:
                                if i == 2:
                                    kv_proj(ti)
                                attn(i - 2, ti)
                        elif kind == "mlp":
                            mlp(i)
                        elif kind == "ple":
                            ple(i, seq, ti)
                    P.dma("sp", lambda e, seq=seq, ti=ti: e.dma_start(
                        out=out_d[seq, ti * TT:(ti + 1) * TT, :].rearrange("(tt p) d -> p tt d", p=128), in_=h[:, :, :]),
                        reads=HALL, final=True)

        emit_all()
        P = Prog(nc)
        state.update({"bank": 0, "pan": 0, "sg": 0, "att": 0, "zb": 0, "recording": False, "issued": 0})
        emit_all()
        P.emit(st)
        build.stats = P.stats
    return nc


def host_layouts(inp):
    f = np.float32
    g = {}
    gl = [inp["a_norm_pre"][0], inp["a_norm_pre"][1], inp["kv_norm"], inp["b_norm_pre"][0], inp["b_norm_pre"][1],
          inp["mlp_norm_pre"][0], inp["mlp_norm_pre"][1], inp["mlp_norm_pre"][2], inp["mlp_norm_pre"][3]]
    gc = np.stack([np.asarray(v, f).reshape(16, 128).T for v in gl], axis=1)
    g["gcols"] = np.ascontiguousarray(gc.reshape(128, 9 * 16))
    g["grows"] = np.ascontiguousarray(np.stack([inp["a_norm_post"][0], inp["a_norm_post"][1], inp["b_norm_post"][0],
                                                inp["b_norm_post"][1], inp["mlp_norm_post"][0],
                                                inp["mlp_norm_post"][1], inp["mlp_norm_post"][2],
                                                inp["mlp_norm_post"][3]]).astype(f))
    ar = np.asarray(inp["ssm_a_re"], f)
    ai = np.asarray(inp["ssm_a_im"], f)
    ls = np.broadcast_to(np.asarray(inp["ssm_log_step"], f)[:, :, None], ar.shape)
    a_all = np.stack([ar, ai, ls], axis=1)
    g["aL1"] = np.ascontiguousarray(a_all.reshape(2, 3, 16, 512))
    g["a3"] = np.ascontiguousarray(a_all.reshape(2, 3, 64, 2, 64).transpose(0, 1, 3, 4, 2).reshape(2, 3, 128, 64))
    b = np.stack([np.asarray(inp["ssm_b_re"], f), np.asarray(inp["ssm_b_im"], f)], axis=1)
    c = np.stack([np.asarray(inp["ssm_c_re"], f), np.asarray(inp["ssm_c_im"], f)], axis=1)
    bL1 = np.zeros((2, 2, 16, 8, 16, 8, 64), f)
    bL2 = np.zeros((2, 2, 16, 2, 64, 4, 8, 16), f)
    cL2 = np.zeros((2, 2, 16, 2, 64, 4, 8, 16), f)
    bj = b.reshape(2, 2, 16, 8, 64, 16)
    cj = c.reshape(2, 2, 16, 8, 16, 64)
    for g8 in range(8):
        bL1[:, :, :, g8, :, g8, :] = bj[:, :, :, g8].transpose(0, 1, 2, 4, 3)
        q, g2 = g8 // 2, g8 % 2
        bL2[:, :, :, g2, :, q, g8, :] = bj[:, :, :, g8]
        cL2[:, :, :, g2, :, q, g8, :] = cj[:, :, :, g8].transpose(0, 1, 2, 4, 3)
    g["bL1"] = bL1.reshape(2, 2, 16, 128, 512)
    g["bL2"] = bL2.reshape(2, 2, 16, 128, 512)
    g["cL2"] = cL2.reshape(2, 2, 16, 128, 512)
    d = np.asarray(inp["ssm_d"], f).reshape(2, 16, 128)
    dm = np.zeros((2, 16, 128, 128), f)
    idx = np.arange(128)
    dm[:, :, idx, idx] = d
    g["dmat"] = dm
    cst = np.zeros((128, 1024), f)
    cst[idx, idx] = 1.0
    jj, ss = np.meshgrid(idx, idx, indexing="ij")
    cst[:, 128:256] = (jj > ss)
    cst[:, 256:384] = (jj <= ss)
    cst[:, 384:896] = np.tile((jj < ss).astype(f), (1, 4))
    g["consts"] = cst
    for k in ("ssm_w_in", "ssm_w_glu", "w_k", "w_v", "w_q", "w_o", "mlp_w1", "mlp_w2", "ple_w", "ple_gate"):
        g[k] = np.ascontiguousarray(np.asarray(inp[k], f))
    return g


def run(inp, S, nseq_total, ncores, plan):
    shared = host_layouts(inp)
    x = np.asarray(inp["x"], np.float32)
    p = np.asarray(inp["p"], np.float32)
    nseq = nseq_total // ncores
    nc = build(S, nseq, plan)
    in_maps = []
    for c in range(ncores):
        m = dict(shared)
        m["x"] = np.ascontiguousarray(x[c * nseq:(c + 1) * nseq])
        m["pT"] = np.ascontiguousarray(p[:, c * nseq:(c + 1) * nseq].transpose(0, 1, 3, 2))
        in_maps.append(m)
    res = run_bass_kernel_spmd(nc, in_maps, core_ids=list(range(ncores)))
    return np.concatenate([r["out"] for r in res.results], axis=0)


def kernel(**inputs):
    return run(inputs, 2048, 16, 8, full_plan())
```

```python
import contextlib
import math
import numpy as np
import concourse.bass as bass
import concourse.mybir as mybir
from concourse.bass_utils import run_bass_kernel_spmd

F32 = mybir.dt.float32
BF16 = mybir.dt.bfloat16
ALU = mybir.AluOpType
AF = mybir.ActivationFunctionType

D = 2048
DFF = 8192
DEPTH = 4
TT = 512
T8 = 8
NCH = TT // T8
EPS = 1e-6
SCALE = 128 ** -0.5


class Op:
    __slots__ = ("eng", "fn", "reads", "writes", "dma", "deps", "signal", "semval")

    def __init__(self, eng, fn, reads, writes, dma):
        self.eng = eng
        self.fn = fn
        self.reads = reads
        self.writes = writes
        self.dma = dma
        self.deps = []
        self.signal = False
        self.semval = None


class Prog:
    SAME_ENGINE_SYNC = ("dve", "pool")
    NDMA = {"sp": 24, "pool": 12, "act": 4}
    ROT = 30000

    def __init__(self, nc):
        self.nc = nc
        self.ops = []
        self.final_dma = []

    def engine(self, name):
        nc = self.nc
        return {"pe": nc.tensor, "act": nc.scalar, "dve": nc.vector, "pool": nc.gpsimd, "sp": nc.sync}[name]

    def op(self, eng, fn, reads=(), writes=()):
        o = Op(eng, fn, tuple(reads), tuple(writes), False)
        self.ops.append(o)
        return o

    def dma(self, eng, fn, reads=(), writes=(), final=False):
        o = Op(eng, fn, tuple(reads), tuple(writes), True)
        self.ops.append(o)
        if final:
            self.final_dma.append(o)
        return o

    def alias(self, new, old):
        self.ops.append(("alias", tuple(new), tuple(old)))

    def emit(self, stack):
        nc = self.nc
        ops = self.ops
        last_write = {}
        reads_since = {}
        engs = ("pe", "act", "dve", "pool", "sp")
        seen = {e: {} for e in engs}
        seen_dma = {e: set() for e in engs}
        for i, o in enumerate(ops):
            if isinstance(o, tuple):
                _, new, old = o
                pend = set()
                for r in old:
                    if r in last_write:
                        pend.add(last_write[r])
                    pend.update(reads_since.get(r, ()))
                for r in new:
                    last_write.pop(r, None)
                    reads_since[r] = sorted(pend)
                continue
            deps = set()
            for r in o.reads:
                lw = last_write.get(r)
                if lw is not None:
                    deps.add(lw)
            for w in o.writes:
                lw = last_write.get(w)
                if lw is not None:
                    deps.add(lw)
                deps.update(reads_since.get(w, ()))
            deps.discard(i)
            best = {}
            final = []
            for d in deps:
                od = ops[d]
                if od.dma:
                    if d in seen_dma[o.eng]:
                        continue
                    seen_dma[o.eng].add(d)
                    final.append(d)
                    od.signal = True
                else:
                    if od.eng == o.eng and not o.dma:
                        if od.eng not in self.SAME_ENGINE_SYNC:
                            continue
                    if seen[o.eng].get(od.eng, -1) >= d:
                        continue
                    if best.get(od.eng, -1) < d:
                        best[od.eng] = d
            for e, d in best.items():
                seen[o.eng][e] = d
                ops[d].signal = True
                final.append(d)
            o.deps = final
            for r in o.reads:
                reads_since.setdefault(r, []).append(i)
            for w in o.writes:
                last_write[w] = i
                reads_since[w] = []
        for o in self.final_dma:
            o.signal = True
        nsig = {e: 0 for e in engs}
        for o in ops:
            if not isinstance(o, tuple) and o.signal and not o.dma:
                nsig[o.eng] += 1
        csems = {}
        for e in engs:
            n = nsig[e] // self.ROT + 1
            csems[e] = [stack.enter_context(nc.semaphore(f"c_{e}_{k}")) for k in range(n)]
        dsems = {e: [stack.enter_context(nc.semaphore(f"d_{e}_{k}")) for k in range(n)]
                 for e, n in self.NDMA.items()}
        ccount = {e: 0 for e in engs}
        dcount = {e: 0 for e in self.NDMA}
        for o in ops:
            if isinstance(o, tuple):
                continue
            eng = self.engine(o.eng)
            for d in o.deps:
                sem, val = ops[d].semval
                eng.wait_ge(sem, val)
            if o.dma:
                n = dcount[o.eng]
                dcount[o.eng] += 1
                pool = dsems[o.eng]
                slot = n % len(pool)
                cnt = n // len(pool) + 1
                if cnt > 1:
                    eng.wait_ge(pool[slot], 16 * (cnt - 1))
                ins = o.fn(eng)
                ins.then_inc(pool[slot], 16)
                o.semval = (pool[slot], 16 * cnt)
            else:
                ins = o.fn(eng)
                if o.signal:
                    c = ccount[o.eng]
                    ccount[o.eng] += 1
                    sem = csems[o.eng][c // self.ROT]
                    ins.then_inc(sem, 1)
                    o.semval = (sem, c % self.ROT + 1)
        sp = nc.sync
        for o in self.final_dma:
            sem, val = o.semval
            sp.wait_ge(sem, val)
        self.stats = dict(n_ops=len(ops), nsig=nsig, ndma=dcount)


def full_plan():
    plan = []
    for i in range(DEPTH):
        plan += [("mix", i), ("mlp", i), ("ple", i)]
    return plan


def build(S, NSEQ, plan):
    NT = S // TT
    nc = bass.Bass("TRN2", target_bir_lowering=False)
    layers_a = sorted({i for (k, i) in plan if k == "mix" and i < 2})

    def din(name, shape):
        return nc.dram_tensor(name, list(shape), F32, kind="ExternalInput").ap()

    x_d = din("x", [NSEQ, S, D])
    pT_d = din("pT", [DEPTH, NSEQ, 256, S])
    w_in_d = din("ssm_w_in", [2, D, D])
    w_glu_d = din("ssm_w_glu", [2, D, 2 * D])
    w_k_d = din("w_k", [D, 512])
    w_v_d = din("w_v", [D, 512])
    w_q_d = din("w_q", [2, D, D])
    w_o_d = din("w_o", [2, D, D])
    w1_d = din("mlp_w1", [DEPTH, D, DFF])
    w2_d = din("mlp_w2", [DEPTH, DFF, D])
    plew_d = din("ple_w", [DEPTH, 256, D])
    gate_d = din("ple_gate", [DEPTH, D, D])
    gcols_d = din("gcols", [128, 9 * 16])
    grows_d = din("grows", [8, D])
    aL1_d = din("aL1", [2, 3, 16, 512])
    bL1_d = din("bL1", [2, 2, 16, 128, 512])
    a3_d = din("a3", [2, 3, 128, 64])
    bL2_d = din("bL2", [2, 2, 16, 128, 512])
    cL2_d = din("cL2", [2, 2, 16, 128, 512])
    dmat_d = din("dmat", [2, 16, 128, 128])
    cst_d = din("consts", [128, 1024])
    out_d = nc.dram_tensor("out", [NSEQ, S, D], F32, kind="ExternalOutput").ap()
    Vd = nc.dram_tensor("Vd", [2, 16, 128, 8192], BF16, kind="Internal").ap()
    Od = nc.dram_tensor("Od", [2, 16, 128, 8192], BF16, kind="Internal").ap()
    BDd = nc.dram_tensor("BDd", [2, 16, 128, 1024], BF16, kind="Internal").ap()
    Gsd = nc.dram_tensor("Gsd", [2, 1024, 128], F32, kind="Internal").ap()

    st = contextlib.ExitStack()
    with st:
        def sb(name, shape, dt):
            return st.enter_context(nc.sbuf_tensor("s_" + name, list(shape), dt))

        h = sb("h", [128, 4, D], F32)
        xnT = sb("xnT", [128, 16, TT], BF16)
        pans = [sb(f"pan{i}", [128, 8, 512], BF16) for i in range(3)]
        mb = sb("mb", [128, 4, D], BF16)
        gbc = sb("gbc", [128, D], F32)
        kTc = sb("kTc", [128, 4, S], BF16)
        vvc = sb("vvc", [128, S // 128, 512], BF16)
        sg = [sb(f"sg{i}", [128, 512], F32) for i in range(2)]
        gcols = sb("gcols", [128, 9 * 16], F32)
        cst = sb("cst", [128, 1024], BF16)
        ident_f = sb("ident_f", [128, 128], F32)
        stat = sb("stat", [128, 64], F32)
        Xst = sb("Xst", [128, 2, 2, 64], F32)
        A8 = sb("A8", [128, 2, 2, 64], F32)
        R = sb("R", [128, 32768], BF16)
        banks = [st.enter_context(nc.psum_tensor(f"ps{i}", [128, 512], F32)) for i in range(8)]

        ident = cst[:, 0:128]
        Umat = cst[:, 128:256]
        Lomat = cst[:, 256:384]
        maskd = cst[:, 384:896]

        HALL = [("h", 0), ("h", 1), ("h", 2), ("h", 3)]
        P = Prog(nc)
        state = {"bank": 0, "pan": 0, "sg": 0, "att": 0, "zb": 0, "recording": True, "plog": [], "issued": 0}

        def nb():
            b = state["bank"]
            state["bank"] = (b + 1) % 8
            return b

        def bres(b):
            return ("ps", b)

        def carve(off, shape, dt):
            n = int(np.prod(shape[1:]))
            if dt == F32:
                v = R[:, off // 2: off // 2 + 2 * n].bitcast(F32)
            else:
                v = R[:, off // 2: off // 2 + n]
            if len(shape) == 3:
                v = v.rearrange("p (a b) -> p a b", a=shape[1])
            elif len(shape) == 4:
                v = v.rearrange("p (a b c) -> p a b c", a=shape[1], b=shape[2])
            elif len(shape) == 5:
                v = v.rearrange("p (a b c d) -> p a b c d", a=shape[1], b=shape[2], c=shape[3])
            return v

        def panel(src, nkc=8):
            n = state["pan"]
            state["pan"] = n + 1
            if state["recording"]:
                state["plog"].append((src, nkc))
                return pans[n % 3], ("pan", n % 3)
            plog = state["plog"]
            while state["issued"] < min(n + 3, len(plog)):
                i = state["issued"]
                state["issued"] += 1
                psrc, pk = plog[i]
                pt = pans[i % 3]
                P.dma("pool", lambda e, pt=pt, psrc=psrc, pk=pk: e.dma_start(
                    out=pt[:, 0:pk, :], in_=psrc.rearrange("(kc p) n -> p kc n", p=128)),
                    writes=[("pan", i % 3)])
            return pans[n % 3], ("pan", n % 3)

        def mm(out, lhsT, rhs, start, stop, reads, writes):
            P.op("pe", lambda e, out=out, lhsT=lhsT, rhs=rhs, start=start, stop=stop: e.matmul(
                out, lhsT, rhs, start=start, stop=stop), reads=reads, writes=writes)

        def load_const():
            P.dma("pool", lambda e: e.dma_start(out=cst[:, :], in_=cst_d),
                  writes=["cst"])
            P.dma("sp", lambda e: e.dma_start(out=ident_f[:, :], in_=cst_d[:, 0:128]), writes=["ident_f"])
            P.dma("sp", lambda e: e.dma_start(out=gcols[:, :], in_=gcols_d), writes=["gcols"])

        def norm_T(gi, do_norm=True):
            for tt in range(4):
                if do_norm:
                    P.op("act", lambda e, tt=tt: e.activation(
                        out=mb[:, tt, :], in_=h[:, tt, :], func=AF.Square, accum_out=stat[:, tt:tt + 1]),
                        reads=[("h", tt)], writes=[("mb", tt), ("stat", tt)])
                    P.op("dve", lambda e, tt=tt: e.tensor_scalar(
                        out=stat[:, 8 + tt:9 + tt], in0=stat[:, tt:tt + 1], scalar1=1.0 / D, scalar2=EPS,
                        op0=ALU.mult, op1=ALU.add), reads=[("stat", tt)], writes=[("rstd", tt)])
                    P.op("act", lambda e, tt=tt: e.activation(
                        out=stat[:, 8 + tt:9 + tt], in_=stat[:, 8 + tt:9 + tt], func=AF.Sqrt),
                        reads=[("rstd", tt)], writes=[("rstd", tt)])
                    P.op("dve", lambda e, tt=tt: e.reciprocal(
                        out=stat[:, 8 + tt:9 + tt], in_=stat[:, 8 + tt:9 + tt]),
                        reads=[("rstd", tt)], writes=[("rstd", tt)])
                    P.op("dve", lambda e, tt=tt: e.tensor_scalar(
                        out=mb[:, tt, :], in0=h[:, tt, :], scalar1=stat[:, 8 + tt:9 + tt], scalar2=None,
                        op0=ALU.mult), reads=[("h", tt), ("rstd", tt)], writes=[("mb", tt)])
                else:
                    P.op("dve", lambda e, tt=tt: e.tensor_copy(out=mb[:, tt, :], in_=h[:, tt, :]),
                         reads=[("h", tt)], writes=[("mb", tt)])
            for kc in range(16):
                b = nb()
                pb = banks[b][:, :].bitcast(BF16)
                for tt in range(4):
                    P.op("pe", lambda e, pb=pb, tt=tt, kc=kc: e.transpose(
                        pb[:, tt * 128:(tt + 1) * 128], mb[:, tt, kc * 128:(kc + 1) * 128], ident),
                        reads=[("mb", tt), "cst"], writes=[bres(b)])
                if gi is not None:
                    sc = gcols[:, gi * 16 + kc: gi * 16 + kc + 1]
                else:
                    sc = 1.0
                if kc % 2 == 0:
                    P.op("act", lambda e, pb=pb, kc=kc, sc=sc: e.activation(
                        out=xnT[:, kc, :], in_=pb[:, 0:512], func=AF.Copy, scale=sc),
                        reads=[bres(b), "gcols"], writes=[("xnT", kc)])
                else:
                    P.op("dve", lambda e, pb=pb, kc=kc, sc=sc: e.tensor_scalar(
                        out=xnT[:, kc, :], in0=pb[:, 0:512], scalar1=sc, scalar2=None, op0=ALU.mult),
                        reads=[bres(b), "gcols"], writes=[("xnT", kc)])

        def proj_fm(w2d, ncols, evac):
            for cb in range(ncols // 512):
                bs = [nb() for _ in range(4)]
                for kh in range(2):
                    pt, pr = panel(w2d[kh * 1024:(kh + 1) * 1024, cb * 512:(cb + 1) * 512])
                    for m in range(4):
                        for kc in range(8):
                            kg = kh * 8 + kc
                            mm(banks[bs[m]][:, :], pt[:, kc, m * 128:(m + 1) * 128], xnT[:, kg, :],
                               kg == 0, kg == 15, [pr, ("xnT", kg)], [bres(bs[m])])
                for m in range(4):
                    evac(cb * 4 + m, bs[m])

        def proj_tm_group(lhs_of, lhs_res_of, w2d, K, col0):
            bs = [nb() for _ in range(4)]
            nkp = K // 1024
            for kp in range(nkp):
                pt, pr = panel(w2d[kp * 1024:(kp + 1) * 1024, col0:col0 + 512])
                for tt in range(4):
                    for kc in range(8):
                        kg = kp * 8 + kc
                        mm(banks[bs[tt]][:, :], lhs_of(kg, tt), pt[:, kc, :],
                           kg == 0, kg == K // 128 - 1, [pr, lhs_res_of(kg)], [bres(bs[tt])])
            return bs

        def load_grow(gri):
            P.dma("sp", lambda e: e.dma_start(out=gbc[:, :], in_=grows_d[gri:gri + 1, :].partition_broadcast(128)),
                  writes=["gbc"])

        def post_norm_add():
            for tt in range(4):
                P.op("act", lambda e, tt=tt: e.activation(
                    out=xnT[:, 0:4, :], in_=mb[:, tt, :].rearrange("p (a b) -> p a b", a=4), func=AF.Square,
                    accum_out=stat[:, 16 + tt:17 + tt]),
                    reads=[("mb", tt)], writes=[("xnT", 0), ("xnT", 1), ("xnT", 2), ("xnT", 3), ("stat2", tt)])
                P.op("dve", lambda e, tt=tt: e.tensor_scalar(
                    out=stat[:, 24 + tt:25 + tt], in0=stat[:, 16 + tt:17 + tt], scalar1=1.0 / D, scalar2=EPS,
                    op0=ALU.mult, op1=ALU.add), reads=[("stat2", tt)], writes=[("rstd2", tt)])
                P.op("act", lambda e, tt=tt: e.activation(
                    out=stat[:, 24 + tt:25 + tt], in_=stat[:, 24 + tt:25 + tt], func=AF.Sqrt),
                    reads=[("rstd2", tt)], writes=[("rstd2", tt)])
                P.op("dve", lambda e, tt=tt: e.reciprocal(
                    out=stat[:, 24 + tt:25 + tt], in_=stat[:, 24 + tt:25 + tt]),
                    reads=[("rstd2", tt)], writes=[("rstd2", tt)])
                P.op("dve", lambda e, tt=tt: e.scalar_tensor_tensor(
                    out=mb[:, tt, :], in0=mb[:, tt, :], scalar=stat[:, 24 + tt:25 + tt], in1=gbc[:, :],
                    op0=ALU.mult, op1=ALU.mult), reads=[("mb", tt), ("rstd2", tt), "gbc"], writes=[("mb", tt)])
                P.op("pool", lambda e, tt=tt: e.tensor_tensor(
                    out=h[:, tt, :], in0=h[:, tt, :], in1=mb[:, tt, :], op=ALU.add),
                    reads=[("h", tt), ("mb", tt)], writes=[("h", tt)])

        def evac_to_mb(fb, bs):
            for tt in range(4):
                b = bs[tt]
                if tt % 2 == 0:
                    P.op("act", lambda e, b=b, tt=tt, fb=fb: e.activation(
                        out=mb[:, tt, fb * 512:(fb + 1) * 512], in_=banks[b][:, :], func=AF.Copy),
                        reads=[bres(b)], writes=[("mb", tt)])
                else:
                    P.op("dve", lambda e, b=b, tt=tt, fb=fb: e.tensor_copy(
                        out=mb[:, tt, fb * 512:(fb + 1) * 512], in_=banks[b][:, :]),
                        reads=[bres(b)], writes=[("mb", tt)])

        hid = carve(0, [128, 64, TT], BF16)

        def mlp(i):
            norm_T(5 + i)
            P.alias([("hid", m) for m in range(64)], ["R"])

            def ev(m, b):
                s_ = sg[state["sg"]]
                sr = ("sg", state["sg"])
                state["sg"] ^= 1
                P.op("act", lambda e, b=b, s_=s_: e.activation(out=s_[:, :], in_=banks[b][:, :], func=AF.Relu),
                     reads=[bres(b)], writes=[sr])
                P.op("dve", lambda e, m=m, s_=s_: e.tensor_tensor(out=hid[:, m, :], in0=s_[:, :], in1=s_[:, :],
                                                                 op=ALU.mult),
                     reads=[sr], writes=[("hid", m)])
            proj_fm(w1_d[i], DFF, ev)
            load_grow(4 + i)
            for fb in range(4):
                bs = proj_tm_group(lambda kg, tt: hid[:, kg, tt * 128:(tt + 1) * 128], lambda kg: ("hid", kg),
                                   w2_d[i], DFF, fb * 512)
                evac_to_mb(fb, bs)
            post_norm_add()
            P.alias(["R"], [("hid", m) for m in range(64)])

        pTt = carve(0, [128, 2, TT], BF16)

        def ple(i, seq, ti):
            norm_T(None, do_norm=False)
            P.alias(["pTt"], ["R"])
            P.dma("pool", lambda e: e.dma_start(
                out=pTt[:, :, :], in_=pT_d[i, seq, :, ti * TT:(ti + 1) * TT].rearrange("(kc p) t -> p kc t", p=128)),
                writes=["pTt"])
            for fb in range(4):
                gb = proj_tm_group(lambda kg, tt: xnT[:, kg, tt * 128:(tt + 1) * 128], lambda kg: ("xnT", kg),
                                   gate_d[i], D, fb * 512)
                eb = [nb() for _ in range(4)]
                pt, pr = panel(plew_d[i][:, fb * 512:(fb + 1) * 512], nkc=2)
                for tt in range(4):
                    for kc in range(2):
                        mm(banks[eb[tt]][:, :], pTt[:, kc, tt * 128:(tt + 1) * 128], pt[:, kc, :],
                           kc == 0, kc == 1, [pr, "pTt"], [bres(eb[tt])])
                for tt in range(4):
                    s_ = sg[state["sg"]]
                    sr = ("sg", state["sg"])
                    state["sg"] ^= 1
                    P.op("act", lambda e, b=gb[tt], s_=s_: e.activation(out=s_[:, :], in_=banks[b][:, :],
                                                                       func=AF.Sigmoid),
                         reads=[bres(gb[tt])], writes=[sr])
                    P.op("dve", lambda e, b=eb[tt], s_=s_: e.tensor_tensor(out=s_[:, :], in0=banks[b][:, :],
                                                                          in1=s_[:, :], op=ALU.mult),
                         reads=[bres(eb[tt]), sr], writes=[sr])
                    P.op("pool", lambda e, tt=tt, fb=fb, s_=s_: e.tensor_tensor(
                        out=h[:, tt, fb * 512:(fb + 1) * 512], in0=h[:, tt, fb * 512:(fb + 1) * 512],
                        in1=s_[:, :], op=ALU.add), reads=[("h", tt), sr], writes=[("h", tt)])
            P.alias(["R"], ["pTt"])

        def cmul(eng, outr, outi, ar, ai, br, bi, t1, t2, res_in, res_out, tag="", xw=(), ari=None, aii=None):
            TTm = ALU.mult
            T1, T2 = "t1" + tag, "t2" + tag
            P.op(eng, lambda e: e.tensor_tensor(out=t1, in0=ar, in1=br, op=TTm), reads=res_in, writes=[T1] + list(xw))
            P.op(eng, lambda e: e.tensor_tensor(out=t2, in0=ai, in1=bi, op=TTm), reads=res_in, writes=[T2] + list(xw))
            P.op(eng, lambda e: e.tensor_tensor(out=outr, in0=t1, in1=t2, op=ALU.subtract), reads=[T1, T2],
                 writes=res_out)
            ar2 = ar if ari is None else ari
            ai2 = ai if aii is None else aii
            P.op(eng, lambda e: e.tensor_tensor(out=t1, in0=ar2, in1=bi, op=TTm), reads=res_in + res_out,
                 writes=[T1] + list(xw))
            P.op(eng, lambda e: e.tensor_tensor(out=t2, in0=ai2, in1=br, op=TTm), reads=res_in + res_out,
                 writes=[T2] + list(xw))
            P.op(eng, lambda e: e.tensor_tensor(out=outi, in0=t1, in1=t2, op=ALU.add), reads=[T1, T2],
                 writes=res_out)

        def lam_calc(ar, ai, ls, lr, li, fr, fi, tA, tB, tC, tD, tE):
            V = "dve"
            rs = ["lamin", "lamout"]
            ws = ["lamout"]

            def o(fn):
                P.op(V, fn, reads=rs, writes=ws)

            def horner(q, z, coefs):
                o(lambda e: e.tensor_scalar(out=q, in0=z, scalar1=coefs[-1], scalar2=None, op0=ALU.mult))
                for c in reversed(coefs[:-1]):
                    o(lambda e, c=c: e.scalar_tensor_tensor(out=q, in0=q, scalar=c, in1=z, op0=ALU.add, op1=ALU.mult))
            fact = [1.0]
            for k in range(1, 20):
                fact.append(fact[-1] * k)
            o(lambda e: e.tensor_scalar(out=tB, in0=ls, scalar1=0.125, scalar2=None, op0=ALU.mult))
            horner(tA, tB, [1.0 / fact[k] for k in range(1, 13)])
            o(lambda e: e.tensor_scalar(out=tA, in0=tA, scalar1=1.0, scalar2=None, op0=ALU.add))
            for _ in range(3):
                o(lambda e: e.tensor_tensor(out=tA, in0=tA, in1=tA, op=ALU.mult))
            o(lambda e: e.tensor_tensor(out=tC, in0=ar, in1=tA, op=ALU.mult))
            horner(tB, tC, [1.0 / fact[k] for k in range(1, 9)])
            o(lambda e: e.tensor_scalar(out=tB, in0=tB, scalar1=1.0, scalar2=None, op0=ALU.add))
            o(lambda e: e.tensor_tensor(out=tA, in0=ai, in1=tA, op=ALU.mult))
            MAGIC = 12582912.0
            o(lambda e: e.tensor_scalar(out=tC, in0=tA, scalar1=1.0 / (2 * math.pi), scalar2=None, op0=ALU.mult))
            o(lambda e: e.tensor_scalar(out=tC, in0=tC, scalar1=MAGIC, scalar2=None, op0=ALU.add))
            o(lambda e: e.tensor_scalar(out=tC, in0=tC, scalar1=-MAGIC, scalar2=None, op0=ALU.add))
            C1 = 6.28125
            C2 = 2 * math.pi - C1
            o(lambda e: e.scalar_tensor_tensor(out=tA, in0=tC, scalar=-C1, in1=tA, op0=ALU.mult, op1=ALU.add))
            o(lambda e: e.scalar_tensor_tensor(out=tA, in0=tC, scalar=-C2, in1=tA, op0=ALU.mult, op1=ALU.add))
            o(lambda e: e.tensor_scalar(out=tA, in0=tA, scalar1=0.5, scalar2=None, op0=ALU.mult))
            o(lambda e: e.tensor_tensor(out=tC, in0=tA, in1=tA, op=ALU.mult))
            horner(tD, tC, [(-1.0) ** k / fact[2 * k + 1] for k in range(1, 8)])
            o(lambda e: e.scalar_tensor_tensor(out=tD, in0=tD, scalar=1.0, in1=tA, op0=ALU.add, op1=ALU.mult))
            horner(tE, tC, [(-1.0) ** k / fact[2 * k] for k in range(1, 9)])
            o(lambda e: e.tensor_scalar(out=tE, in0=tE, scalar1=1.0, scalar2=None, op0=ALU.add))
            o(lambda e: e.tensor_tensor(out=li, in0=tD, in1=tE, op=ALU.mult))
            o(lambda e: e.scalar_tensor_tensor(out=li, in0=li, scalar=2.0, in1=tB, op0=ALU.mult, op1=ALU.mult))
            o(lambda e: e.tensor_tensor(out=lr, in0=tD, in1=tD, op=ALU.mult))
            o(lambda e: e.tensor_scalar(out=lr, in0=lr, scalar1=-2.0, scalar2=1.0, op0=ALU.mult, op1=ALU.add))
            o(lambda e: e.tensor_tensor(out=lr, in0=lr, in1=tB, op=ALU.mult))
            o(lambda e: e.tensor_tensor(out=tA, in0=ar, in1=ar, op=ALU.mult))
            o(lambda e: e.tensor_tensor(out=tB, in0=ai, in1=ai, op=ALU.mult))
            o(lambda e: e.tensor_tensor(out=tA, in0=tA, in1=tB, op=ALU.add))
            o(lambda e: e.reciprocal(out=tA, in_=tA))
            o(lambda e: e.tensor_scalar(out=tB, in0=lr, scalar1=-1.0, scalar2=None, op0=ALU.add))
            o(lambda e: e.tensor_tensor(out=fr, in0=tB, in1=ar, op=ALU.mult))
            o(lambda e: e.tensor_tensor(out=tC, in0=li, in1=ai, op=ALU.mult))
            o(lambda e: e.tensor_tensor(out=fr, in0=fr, in1=tC, op=ALU.add))
            o(lambda e: e.tensor_tensor(out=fr, in0=fr, in1=tA, op=ALU.mult))
            o(lambda e: e.tensor_tensor(out=fi, in0=li, in1=ar, op=ALU.mult))
            o(lambda e: e.tensor_tensor(out=tC, in0=tB, in1=ai, op=ALU.mult))
            o(lambda e: e.tensor_tensor(out=fi, in0=fi, in1=tC, op=ALU.subtract))
            o(lambda e: e.tensor_tensor(out=fi, in0=fi, in1=tA, op=ALU.mult))

        def s5_prep(l):
            P.alias(["prep", "lamin", "lamout", "t1", "t2", "t1p", "t2p", "G", "Bt", "Vt", "Ot", "Xt", "Ct", "BDt",
                     "wt", "Cb", "dm", "stg", ("Gk", 0), ("Gk", 1)], ["R"])
            KB = 1024
            o = 0
            Gs = carve(o, [128, 2, 8, 64], F32); o += 4096
            Hs = carve(o, [128, 2, 8, 64], F32); o += 4096
            base = o
            a3 = carve(o, [128, 3, 64], F32); o += 768
            sm = carve(o, [128, 9, 64], F32); o += 2304
            tt1 = carve(o, [128, 64], F32); o += 256
            tt2 = carve(o, [128, 64], F32); o += 256
            stg = carve(o, [128, 512], F32); o += 2048
            P.dma("sp", lambda e: e.dma_start(out=a3[:, :, :], in_=a3_d[l].rearrange("k p n -> p k n")),
                  writes=["lamin"])
            lam_calc(a3[:, 0, :], a3[:, 1, :], a3[:, 2, :], sm[:, 0, :], sm[:, 1, :], sm[:, 2, :], sm[:, 3, :],
                     sm[:, 4, :], sm[:, 5, :], sm[:, 6, :], sm[:, 7, :], sm[:, 8, :])
            P.op("dve", lambda e: e.tensor_copy(out=Gs[:, 0, 0, :], in_=sm[:, 2, :]), reads=["lamout"], writes=["G"])
            P.op("dve", lambda e: e.tensor_copy(out=Gs[:, 1, 0, :], in_=sm[:, 3, :]), reads=["lamout"], writes=["G"])
            P.op("dve", lambda e: e.tensor_copy(out=Hs[:, 0, 0, :], in_=sm[:, 0, :]), reads=["lamout"], writes=["G"])
            P.op("dve", lambda e: e.tensor_copy(out=Hs[:, 1, 0, :], in_=sm[:, 1, :]), reads=["lamout"], writes=["G"])
            for k in range(7):
                cmul("dve", Gs[:, 0, k + 1, :], Gs[:, 1, k + 1, :], sm[:, 0, :], sm[:, 1, :], Gs[:, 0, k, :],
                     Gs[:, 1, k, :], tt1, tt2, ["lamout", "G"], ["G"])
                cmul("dve", Hs[:, 0, k + 1, :], Hs[:, 1, k + 1, :], sm[:, 0, :], sm[:, 1, :], Hs[:, 0, k, :],
                     Hs[:, 1, k, :], tt1, tt2, ["lamout", "G"], ["G"])
            P.alias(["Hsn"], [("xnT", 0), ("xnT", 1), ("xnT", 2), ("xnT", 3)])
            Hsn = xnT[:, 0:4, :].rearrange("p a b -> p (a b)").bitcast(F32).rearrange("p (a b c) -> p a b c", a=2, b=8)
            P.op("dve", lambda e: e.tensor_scalar(out=Hsn, in0=Hs, scalar1=-1.0, scalar2=None, op0=ALU.mult),
                 reads=["G"], writes=["Hsn"])
            P.op("dve", lambda e: e.tensor_copy(out=A8[:, l, 0, :], in_=Hs[:, 0, 7, :]), reads=["G"], writes=["A8"])
            P.op("dve", lambda e: e.tensor_copy(out=A8[:, l, 1, :], in_=Hs[:, 1, 7, :]), reads=["G"], writes=["A8"])
            Gs2 = Gs.rearrange("p a b c -> p (a b c)")
            for half in range(2):
                b = nb()
                for q in range(4):
                    blk = half * 4 + q
                    P.op("pe", lambda e, b=b, q=q, blk=blk: e.transpose(
                        banks[b][:, q * 128:(q + 1) * 128], Gs2[:, blk * 128:(blk + 1) * 128], ident_f[:, :]),
                        reads=["G", "ident_f"], writes=[bres(b)])
                P.op("dve", lambda e, b=b: e.tensor_copy(out=stg[:, :], in_=banks[b][:, :]),
                     reads=[bres(b)], writes=["stg"])
                P.dma("sp", lambda e, half=half: e.dma_start(
                    out=Gsd[l, half * 512:(half + 1) * 512, :].rearrange("(q p) c -> p q c", p=128),
                    in_=stg.rearrange("p (q c) -> p q c", q=4)), reads=["stg"], writes=["Gsd"])
            PJ = ["Bt", ("Ct", 0), ("Ct", 1), "Cb", "Vt", ("Xt", 0), ("Xt", 1), "BDt", "dm", ("Gk", 0), ("Gk", 1),
                  "t1", "t2", "t1p", "t2p"]
            P.alias(PJ, ["lamin", "lamout", "stg", "t1", "t2"])
            HN = [("Ot", 0), ("Ot", 1)]
            P.alias(HN, HALL)
            hb = h.rearrange("p a b -> p (a b)")
            Otb = [hb[:, i * 4096:(i + 1) * 4096].bitcast(BF16).rearrange("p (a b c d) -> p a b c d", a=4, b=8, c=2)
                   for i in range(2)]
            o = base
            Ct2 = [carve(o + i * 4 * KB, [128, 2, 4, 128], F32) for i in range(2)]; o += 8 * KB
            Cb = carve(o, [128, 2, 4, 128], BF16); o += 2 * KB
            BDt = carve(o, [128, 8, 128], BF16); o += 2 * KB
            dm = carve(o, [128, 128], F32); o += KB // 2
            Xt = [carve(o + i * 2 * KB, [128, 2, 4, 128], BF16) for i in range(2)]; o += 4 * KB
            Bt = carve(o, [128, 2, 512], F32); o += 4 * KB
            Gk = [carve(o + i * 4 * KB, [128, 2, 512], F32) for i in range(2)]; o += 8 * KB
            tq1 = carve(o, [128, 512], F32); o += 2 * KB
            tq2 = carve(o, [128, 512], F32); o += 2 * KB
            Vt = carve(o, [128, 4, 8, 2, 128], BF16); o += 16 * KB
            u1 = carve(o, [128, 4, 128], F32); o += 2 * KB
            u2 = carve(o, [128, 4, 128], F32); o += 2 * KB
            for j in range(16):
                Ot = Otb[j % 2]
                otr = ("Ot", j % 2)
                ct_ = Ct2[j % 2]
                cr_ = ("Ct", j % 2)
                P.dma("sp", lambda e, j=j: e.dma_start(
                    out=Bt[:, :, :], in_=bL1_d[l, :, j, :, :].rearrange("k p n -> p k n")), writes=["Bt"])
                P.dma("sp", lambda e, ct_=ct_, j=j: e.dma_start(
                    out=ct_.rearrange("p k q c -> p k (q c)"), in_=cL2_d[l, :, j, :, :].rearrange("k p n -> p k n")),
                    writes=[cr_])
                P.dma("sp", lambda e, j=j: e.dma_start(out=dm[:, :], in_=dmat_d[l, j]), writes=["dm"])
                P.op("dve", lambda e, ct_=ct_: e.tensor_copy(out=Cb[:, 0, :, :], in_=ct_[:, 0, :, :]),
                     reads=[cr_], writes=["Cb"])
                P.op("dve", lambda e, ct_=ct_: e.tensor_scalar(out=Cb[:, 1, :, :], in0=ct_[:, 1, :, :],
                                                               scalar1=-1.0, scalar2=None, op0=ALU.mult),
                     reads=[cr_], writes=["Cb"])
                for k in range(8):
                    s_ = 7 - k
                    gb_ = Gk[k % 2]
                    gr_ = ("Gk", k % 2)
                    xt_ = Xt[k % 2]
                    xr_ = ("Xt", k % 2)
                    P.dma("sp", lambda e, gb_=gb_, j=j, k=k: e.dma_start(
                        out=gb_[:, :, :],
                        in_=Gsd[l].rearrange("(r m) c -> r (m c)", r=2)[:, (k * 64 + 4 * j) * 128:(k * 64 + 4 * j + 4) * 128]
                        .partition_broadcast(128)), reads=["Gsd"], writes=[gr_])
                    v4 = lambda a: a.rearrange("p (q m) -> p q m", q=4)
                    cmul("dve", Vt[:, :, s_, 0, :], Vt[:, :, s_, 1, :], v4(gb_[:, 0, :]), v4(gb_[:, 1, :]),
                         v4(Bt[:, 0, :]), v4(Bt[:, 1, :]), v4(tq1), v4(tq2), [gr_, "Bt"], ["Vt"])
                    tbk = nb()
                    pb = banks[tbk][:, :].bitcast(BF16)
                    for ri in range(2):
                        for q in range(4):
                            P.op("pe", lambda e, pb=pb, ri=ri, q=q, s_=s_: e.transpose(
                                pb[:, (ri * 4 + q) * 128:(ri * 4 + q + 1) * 128], Vt[:, q, s_, ri, :], ident),
                                reads=["Vt", "cst"], writes=[bres(tbk)])
                    P.op("act", lambda e, pb=pb, xt_=xt_: e.activation(
                        out=xt_.rearrange("p r q c -> p (r q c)"), in_=pb[:, :], func=AF.Copy),
                        reads=[bres(tbk)], writes=[xr_])
                    if k % 4 == 0:
                        b = nb()
                    kk = k % 4
                    n = 0
                    for q in range(4):
                        for ri in range(2):
                            mm(banks[b][:, kk * 128:(kk + 1) * 128], xt_[:, ri, q, :], Cb[:, ri, q, :],
                               (kk == 0 and n == 0), (kk == 3 and n == 7), [xr_, "Cb"], [bres(b)])
                            n += 1
                    if kk == 3:
                        k4 = k // 4
                        if k4 == 0:
                            P.op("dve", lambda e, b=b: e.tensor_tensor(
                                out=banks[b][:, 0:128], in0=banks[b][:, 0:128], in1=dm[:, :], op=ALU.add),
                                reads=[bres(b), "dm"], writes=[bres(b)])
                        P.op("act", lambda e, b=b, k4=k4: e.activation(
                            out=BDt[:, 4 * k4:4 * k4 + 4, :], in_=banks[b][:, :].rearrange("p (k c) -> p k c", k=4),
                            func=AF.Copy), reads=[bres(b)], writes=["BDt"])
                P.dma("sp", lambda e, j=j: e.dma_start(
                    out=Vd[l, j], in_=Vt.rearrange("p a b c d -> p (a b c d)")), reads=["Vt"], writes=["Vd"])
                P.dma("sp", lambda e, j=j: e.dma_start(
                    out=BDd[l, j], in_=BDt.rearrange("p a b -> p (a b)")), reads=["BDt"], writes=["BDd"])

                def bc(small, ri, k, j=j):
                    return small[:, ri, k, 4 * j:4 * j + 4].unsqueeze(2).to_broadcast([128, 4, 128])
                for k in range(8):
                    cmul("pool", Ot[:, :, k, 0, :], Ot[:, :, k, 1, :], bc(Hs, 0, k), bc(Hs, 1, k), ct_[:, 0, :, :],
                         ct_[:, 1, :, :], u1, u2, ["G", "Hsn", cr_], [otr], tag="p", ari=bc(Hsn, 0, k), aii=bc(Hsn, 1, k))
                P.dma("sp", lambda e, j=j, Ot=Ot: e.dma_start(
                    out=Od[l, j], in_=Ot.rearrange("p a b c d -> p (a b c d)")), reads=[otr], writes=["Od"])
            P.alias(HALL, HN)
            P.alias([("xnT", 0), ("xnT", 1), ("xnT", 2), ("xnT", 3)], ["Hsn"])
            P.alias(["R"], ["prep", "lamin", "lamout", "G", "stg"] + PJ)

        def s5(l, first_tile):
            KB = 1024
            o = 0
            uT = carve(o, [128, 16, TT], BF16); o += 16 * KB
            VO = [carve(o + k * 4 * KB, [128, 8, 2, 128], BF16) for k in range(4)]; o += 16 * KB
            BDb = [carve(o + k * 2 * KB, [128, 8, 128], BF16) for k in range(2)]; o += 4 * KB
            Sb0 = carve(o, [128, NCH, 2, 32], F32); o += 16 * KB
            Xp0 = carve(o, [128, 2, 32, NCH], BF16); o += 8 * KB
            sc = carve(o, [128, 2, 2, 32], F32); o += KB
            Sb1 = xnT.rearrange("p a b -> p (a b)").bitcast(F32).rearrange("p (c r q) -> p c r q", r=2, q=32)
            Xp1 = mb[:, 0:2, :].rearrange("p a b -> p (a b)").rearrange("p (r q c) -> p r q c", r=2, q=32)
            Sbs = [Sb0, Sb1]
            Xps = [Xp0, Xp1]
            sbn = [[("Sbc", hf, c) for c in range(NCH)] for hf in range(2)]
            names = ([("fa", j) for j in range(16)] + [("VO", k) for k in range(4)]
                     + [("BDb", k) for k in range(2)] + sbn[0] + [("Xp", 0), "st", "su0", "su1"])
            xn_all = [("xnT", kc) for kc in range(16)]
            norm_T(l)
            P.alias(names, ["R"])

            def ev_u(m, b):
                dst = uT[:, m, :].rearrange("p (s c) -> p c s", s=T8)
                src = banks[b][:, :].rearrange("p (c s) -> p c s", s=T8)
                if m % 2 == 0:
                    P.op("act", lambda e, dst=dst, src=src: e.activation(out=dst, in_=src, func=AF.Copy),
                         reads=[bres(b)], writes=[("fa", m)])
                else:
                    P.op("dve", lambda e, dst=dst, src=src: e.tensor_copy(out=dst, in_=src),
                         reads=[bres(b)], writes=[("fa", m)])
            proj_fm(w_in_d[l], D, ev_u)
            P.alias(sbn[1], xn_all)
            P.alias([("Xp", 1)], [("mb", 0), ("mb", 1)])
            if first_tile:
                P.op("dve", lambda e: e.memset(Xst[:, l, :, :], 0.0), writes=[("Xst", l)])
            cnt = {"vo": 0, "bd": 0}
            for half in range(2):
                Sb = Sbs[half]
                for jj in range(8):
                    j = half * 8 + jj
                    b = nb()
                    for q in range(4):
                        k = cnt["vo"] % 4
                        cnt["vo"] += 1
                        P.dma("sp", lambda e, k=k, j=j, q=q: e.dma_start(
                            out=VO[k].rearrange("p a b c -> p (a b c)"), in_=Vd[l, j][:, q * 2048:(q + 1) * 2048]),
                            reads=["Vd"], writes=[("VO", k)])
                        for ri in range(2):
                            for s in range(T8):
                                mm(banks[b][:, (q * 2 + ri) * NCH:(q * 2 + ri + 1) * NCH], VO[k][:, s, ri, :],
                                   uT[:, j, s * NCH:(s + 1) * NCH], (q == 0 and ri == 0 and s == 0),
                                   (q == 3 and ri == 1 and s == T8 - 1),
                                   [("VO", k), ("fa", j)], [bres(b)])
                    P.op("act", lambda e, b=b, jj=jj, Sb=Sb: e.activation(
                        out=Sb[:, :, :, 4 * jj:4 * jj + 4].rearrange("p c r q -> p q r c"),
                        in_=banks[b][:, :].rearrange("p (q r c) -> p q r c", q=4, r=2), func=AF.Copy),
                        reads=[bres(b)], writes=sbn[half])
            for half in range(2):
                Sb = Sbs[half]
                Xp = Xps[half]
                Ar2 = A8[:, l, 0, 32 * half:32 * half + 32].unsqueeze(1).to_broadcast([128, 2, 32])
                Ai = A8[:, l, 1, 32 * half:32 * half + 32]
                for c in range(NCH):
                    if c == 0:
                        Xprev = Xst[:, l, :, 32 * half:32 * half + 32]
                        pres = ("Xst", l)
                    else:
                        Xprev = Sb[:, c - 1, :, :]
                        pres = ("Sbc", half, c - 1)
                    P.op("dve", lambda e, Xprev=Xprev, Ar2=Ar2: e.tensor_tensor(out=sc[:, 0, :, :], in0=Xprev, in1=Ar2,
                                                                                op=ALU.mult),
                         reads=[pres, "A8"], writes=["st"])
                    P.op("dve", lambda e, Xprev=Xprev, Ai=Ai: e.scalar_tensor_tensor(
                        out=sc[:, 1, 0, :], in0=Xprev[:, 1, :], scalar=-1.0, in1=Ai, op0=ALU.mult, op1=ALU.mult),
                        reads=[pres, "A8"], writes=["su0"])
                    P.op("dve", lambda e, Xprev=Xprev, Ai=Ai: e.tensor_tensor(out=sc[:, 1, 1, :], in0=Xprev[:, 0, :],
                                                                              in1=Ai, op=ALU.mult),
                         reads=[pres, "A8"], writes=["su1"])
                    P.op("dve", lambda e, c=c, Sb=Sb: e.tensor_tensor(out=Sb[:, c, :, :], in0=Sb[:, c, :, :],
                                                                      in1=sc[:, 0, :, :], op=ALU.add),
                         reads=["st", ("Sbc", half, c)], writes=[("Sbc", half, c)])
                    P.op("dve", lambda e, c=c, Sb=Sb: e.tensor_tensor(out=Sb[:, c, :, :], in0=Sb[:, c, :, :],
                                                                      in1=sc[:, 1, :, :], op=ALU.add),
                         reads=["su0", "su1", ("Sbc", half, c)], writes=[("Sbc", half, c)])
                P.op("act", lambda e, half=half, Xp=Xp: e.activation(out=Xp[:, :, :, 0],
                                                                     in_=Xst[:, l, :, 32 * half:32 * half + 32],
                                                                     func=AF.Copy), reads=[("Xst", l)], writes=[("Xp", half)])
                P.op("act", lambda e, Xp=Xp, Sb=Sb: e.activation(out=Xp[:, :, :, 1:NCH],
                                                                 in_=Sb[:, 0:NCH - 1, :, :].rearrange("p c r q -> p r q c"),
                                                                 func=AF.Copy), reads=sbn[half], writes=[("Xp", half)])
                P.op("dve", lambda e, half=half, Sb=Sb: e.tensor_copy(out=Xst[:, l, :, 32 * half:32 * half + 32],
                                                                      in_=Sb[:, NCH - 1, :, :]),
                     reads=sbn[half] + [("Xp", half)], writes=[("Xst", l)])
            for half in range(2):
                Xp = Xps[half]
                for jj in range(8):
                    j = half * 8 + jj
                    kb = cnt["bd"] % 2
                    cnt["bd"] += 1
                    P.dma("sp", lambda e, kb=kb, j=j: e.dma_start(
                        out=BDb[kb].rearrange("p a b -> p (a b)"), in_=BDd[l, j]),
                        reads=["BDd"], writes=[("BDb", kb)])
                    b = nb()
                    for t in range(T8):
                        for s in range(t + 1):
                            mm(banks[b][:, t * NCH:(t + 1) * NCH], BDb[kb][:, t - s, :], uT[:, j, s * NCH:(s + 1) * NCH],
                               (t == 0 and s == 0), False, [("BDb", kb), ("fa", j)], [bres(b)])
                    for q in range(4):
                        k = cnt["vo"] % 4
                        cnt["vo"] += 1
                        P.dma("sp", lambda e, k=k, j=j, q=q: e.dma_start(
                            out=VO[k].rearrange("p a b c -> p (a b c)"), in_=Od[l, j][:, q * 2048:(q + 1) * 2048]),
                            reads=["Od"], writes=[("VO", k)])
                        for t in range(T8):
                            for ri in range(2):
                                mm(banks[b][:, t * NCH:(t + 1) * NCH], VO[k][:, t, ri, :],
                                   Xp[:, ri, 4 * jj + q, :], False, (q == 3 and ri == 1 and t == T8 - 1),
                                   [("VO", k), ("Xp", half)], [bres(b)])
                    P.op("act", lambda e, b=b, j=j: e.activation(
                        out=uT[:, j, :].rearrange("p (c t) -> p t c", t=T8),
                        in_=banks[b][:, :].rearrange("p (t c) -> p t c", t=T8), func=AF.Gelu_apprx_tanh),
                        reads=[bres(b)], writes=[("fa", j)])
            P.alias(xn_all, sbn[1])
            P.alias([("mb", 0), ("mb", 1)], [("Xp", 1)])
            load_grow(l)
            for fb in range(4):
                vb_ = proj_tm_group(lambda kg, tt: uT[:, kg, tt * 128:(tt + 1) * 128], lambda kg: ("fa", kg),
                                    w_glu_d[l], D, fb * 512)
                gb_ = proj_tm_group(lambda kg, tt: uT[:, kg, tt * 128:(tt + 1) * 128], lambda kg: ("fa", kg),
                                    w_glu_d[l], D, D + fb * 512)
                for tt in range(4):
                    s_ = sg[state["sg"]]
                    sr = ("sg", state["sg"])
                    state["sg"] ^= 1
                    P.op("act", lambda e, b=gb_[tt], s_=s_: e.activation(out=s_[:, :], in_=banks[b][:, :],
                                                                        func=AF.Sigmoid),
                         reads=[bres(gb_[tt])], writes=[sr])
                    P.op("dve", lambda e, b=vb_[tt], s_=s_, tt=tt, fb=fb: e.tensor_tensor(
                        out=mb[:, tt, fb * 512:(fb + 1) * 512], in0=banks[b][:, :], in1=s_[:, :], op=ALU.mult),
                        reads=[bres(vb_[tt]), sr], writes=[("mb", tt)])
            post_norm_add()
            P.alias(["R"], names)

        def kv_proj(ti):
            norm_T(2)

            def ev_k(m, b):
                P.op("act", lambda e, b=b, m=m: e.activation(out=kTc[:, m, ti * TT:(ti + 1) * TT], in_=banks[b][:, :],
                                                            func=AF.Copy), reads=[bres(b)], writes=["kTc"])
            proj_fm(w_k_d, 512, ev_k)
            bs = proj_tm_group(lambda kg, tt: xnT[:, kg, tt * 128:(tt + 1) * 128], lambda kg: ("xnT", kg),
                               w_v_d, D, 0)
            for tt in range(4):
                P.op("dve", lambda e, b=bs[tt], tt=tt: e.tensor_copy(out=vvc[:, ti * 4 + tt, :], in_=banks[b][:, :]),
                     reads=[bres(bs[tt])], writes=["vvc"])

        def attn(jb, ti):
            KB = 1024
            o = 0
            qT = carve(o, [128, 16, TT], BF16); o += 16 * KB
            eb = [carve(o + k * 2 * KB, [128, 512], F32) for k in range(4)]; o += 8 * KB
            Lp = [carve(o + k * KB, [128, 512], BF16) for k in range(8)]; o += 8 * KB
            tb = [carve(o + k * 2 * KB, [128, 512], F32) for k in range(8)]; o += 16 * KB
            wT = [carve(o + k * KB, [128, 512], BF16) for k in range(6)]; o += 6 * KB
            names = ([("fa", j) for j in range(16)] + [("eb", k) for k in range(4)] + [("Lp", k) for k in range(8)]
                     + [("tb", k) for k in range(8)] + [("wT", k) for k in range(6)])
            norm_T(3 + jb)
            P.alias(names, ["R"])

            def ev_q(m, b):
                if m % 2 == 0:
                    P.op("act", lambda e, b=b, m=m: e.activation(out=qT[:, m, :], in_=banks[b][:, :], func=AF.Copy),
                         reads=[bres(b)], writes=[("fa", m)])
                else:
                    P.op("dve", lambda e, b=b, m=m: e.tensor_copy(out=qT[:, m, :], in_=banks[b][:, :]),
                         reads=[bres(b)], writes=[("fa", m)])
            proj_fm(w_q_d[jb], D, ev_q)
            streams = [[], []]
            for qb in range(4):
                nkb = ti * 4 + qb + 1
                for kvh in range(4):
                    st_ = kvh % 2
                    for idx, kb in enumerate(range(nkb - 1, -1, -1)):
                        streams[st_].append(dict(qb=qb, kvh=kvh, kb=kb, first=(idx == 0), last=(kb == 0),
                                                 diag=(kb == nkb - 1), acc=st_, ob=2 + st_, s=st_,
                                                 i=len(streams[st_])))

            def qview(u):
                return qT[:, 4 * u["kvh"]:4 * u["kvh"] + 4, u["qb"] * 128:(u["qb"] + 1) * 128]

            def hres_of(u):
                return [("fa", 4 * u["kvh"] + g) for g in range(4)]

            def stageA(u):
                i = u["i"]
                zb = 4 + 2 * u["s"] + i % 2
                e2, l4 = 2 * u["s"] + i % 2, 4 * u["s"] + i % 4
                kb, kvh = u["kb"], u["kvh"]
                mm(banks[zb][:, :].rearrange("p (g q) -> p g q", g=4),
                   kTc[:, kvh, kb * 128:(kb + 1) * 128], qview(u), True, True, ["kTc"] + hres_of(u), [bres(zb)])
                P.op("act", lambda e, zb=zb, e2=e2: e.activation(out=eb[e2][:, :], in_=banks[zb][:, :],
                                                                 func=AF.Exp, scale=SCALE),
                     reads=[bres(zb)], writes=[("eb", e2)])
                P.op("act", lambda e, e2=e2, l4=l4: e.activation(out=Lp[l4][:, :], in_=eb[e2][:, :], func=AF.Ln,
                                                                 bias=1.0),
                     reads=[("eb", e2)], writes=[("Lp", l4)])
                if u["diag"]:
                    P.op("dve", lambda e, l4=l4: e.tensor_tensor(out=Lp[l4][:, :], in0=Lp[l4][:, :],
                                                                 in1=maskd, op=ALU.mult),
                         reads=[("Lp", l4), "cst"], writes=[("Lp", l4)])
                P.op("dve", lambda e, zb=zb, l4=l4: e.scalar_tensor_tensor(
                    out=tb[l4][:, :], in0=banks[zb][:, :], scalar=SCALE, in1=Lp[l4][:, :],
                    op0=ALU.mult, op1=ALU.subtract), reads=[bres(zb), ("Lp", l4)], writes=[("tb", l4)])

            def stageB(u):
                i = u["i"]
                l4, w2_ = 4 * u["s"] + i % 4, 3 * u["s"] + i % 3
                acc = u["acc"]
                mm(banks[acc][:, :], Umat, Lp[l4][:, :], u["first"], False, ["cst", ("Lp", l4)], [bres(acc)])
                P.op("dve", lambda e, acc=acc, l4=l4: e.tensor_tensor(
                    out=tb[l4][:, :], in0=tb[l4][:, :], in1=banks[acc][:, :], op=ALU.subtract),
                    reads=[("tb", l4), bres(acc)], writes=[("tb", l4)])
                if not u["last"]:
                    mm(banks[acc][:, :], Lomat, Lp[l4][:, :], False, False, ["cst", ("Lp", l4)], [bres(acc)])
                P.op("act", lambda e, l4=l4, w2_=w2_: e.activation(out=wT[w2_][:, :], in_=tb[l4][:, :], func=AF.Exp),
                     reads=[("tb", l4)], writes=[("wT", w2_)])
                if u["diag"]:
                    P.op("dve", lambda e, w2_=w2_: e.tensor_tensor(out=wT[w2_][:, :], in0=wT[w2_][:, :],
                                                                   in1=maskd, op=ALU.mult),
                         reads=[("wT", w2_), "cst"], writes=[("wT", w2_)])

            def stageC(u):
                i = u["i"]
                w2_ = 3 * u["s"] + i % 3
                ob = u["ob"]
                kb, kvh = u["kb"], u["kvh"]
                mm(banks[ob][:, :], vvc[:, kb, kvh * 128:(kvh + 1) * 128], wT[w2_][:, :], u["first"], u["last"],
                   ["vvc", ("wT", w2_)], [bres(ob)])
                if u["last"]:
                    qv = qview(u)
                    P.op("act", lambda e, ob=ob, qv=qv: e.activation(
                        out=qv, in_=banks[ob][:, :].rearrange("p (g q) -> p g q", g=4), func=AF.Copy),
                        reads=[bres(ob)], writes=hres_of(u))
            nu = len(streams[0])
            for i in range(min(2, nu)):
                for st_ in range(2):
                    stageA(streams[st_][i])
            for i in range(nu):
                for st_ in range(2):
                    stageB(streams[st_][i])
                if i + 2 < nu:
                    for st_ in range(2):
                        stageA(streams[st_][i + 2])
                if i >= 1:
                    for st_ in range(2):
                        stageC(streams[st_][i - 1])
            for st_ in range(2):
                stageC(streams[st_][nu - 1])
            load_grow(2 + jb)
            for fb in range(4):
                bs = proj_tm_group(lambda kg, tt: qT[:, kg, tt * 128:(tt + 1) * 128], lambda kg: ("fa", kg),
                                   w_o_d[jb], D, fb * 512)
                evac_to_mb(fb, bs)
            post_norm_add()
            P.alias(["R"], names)

        def emit_all():
            load_const()
            P.op("dve", lambda e: e.memset(stat[:, :], 0.0), writes=["statinit"])
            for l in layers_a:
                s5_prep(l)
            for seq in range(NSEQ):
                for ti in range(NT):
                    P.dma("sp", lambda e, seq=seq, ti=ti: e.dma_start(
                        out=h[:, :, :], in_=x_d[seq, ti * TT:(ti + 1) * TT, :].rearrange("(tt p) d -> p tt d", p=128)),
                        writes=HALL)
                    for (kind, i) in plan:
                        if kind == "mix":
                            if i < 2:
                                s5(i, ti == 0)
                            else:
                                if i == 2:
                                    kv_proj(ti)
                                attn(i - 2, ti)
                        elif kind == "mlp":
                            mlp(i)
                        elif kind == "ple":
                            ple(i, seq, ti)
                    P.dma("sp", lambda e, seq=seq, ti=ti: e.dma_start(
                        out=out_d[seq, ti * TT:(ti + 1) * TT, :].rearrange("(tt p) d -> p tt d", p=128), in_=h[:, :, :]),
                        reads=HALL, final=True)

        emit_all()
        P = Prog(nc)
        state.update({"bank": 0, "pan": 0, "sg": 0, "att": 0, "zb": 0, "recording": False, "issued": 0})
        emit_all()
        P.emit(st)
        build.stats = P.stats
    return nc


def host_layouts(inp):
    f = np.float32
    g = {}
    gl = [inp["a_norm_pre"][0], inp["a_norm_pre"][1], inp["kv_norm"], inp["b_norm_pre"][0], inp["b_norm_pre"][1],
          inp["mlp_norm_pre"][0], inp["mlp_norm_pre"][1], inp["mlp_norm_pre"][2], inp["mlp_norm_pre"][3]]
    gc = np.stack([np.asarray(v, f).reshape(16, 128).T for v in gl], axis=1)
    g["gcols"] = np.ascontiguousarray(gc.reshape(128, 9 * 16))
    g["grows"] = np.ascontiguousarray(np.stack([inp["a_norm_post"][0], inp["a_norm_post"][1], inp["b_norm_post"][0],
                                                inp["b_norm_post"][1], inp["mlp_norm_post"][0],
                                                inp["mlp_norm_post"][1], inp["mlp_norm_post"][2],
                                                inp["mlp_norm_post"][3]]).astype(f))
    ar = np.asarray(inp["ssm_a_re"], f)
    ai = np.asarray(inp["ssm_a_im"], f)
    ls = np.broadcast_to(np.asarray(inp["ssm_log_step"], f)[:, :, None], ar.shape)
    a_all = np.stack([ar, ai, ls], axis=1)
    g["aL1"] = np.ascontiguousarray(a_all.reshape(2, 3, 16, 512))
    g["a3"] = np.ascontiguousarray(a_all.reshape(2, 3, 64, 2, 64).transpose(0, 1, 3, 4, 2).reshape(2, 3, 128, 64))
    b = np.stack([np.asarray(inp["ssm_b_re"], f), np.asarray(inp["ssm_b_im"], f)], axis=1)
    c = np.stack([np.asarray(inp["ssm_c_re"], f), np.asarray(inp["ssm_c_im"], f)], axis=1)
    bL1 = np.zeros((2, 2, 16, 8, 16, 8, 64), f)
    bL2 = np.zeros((2, 2, 16, 2, 64, 4, 8, 16), f)
    cL2 = np.zeros((2, 2, 16, 2, 64, 4, 8, 16), f)
    bj = b.reshape(2, 2, 16, 8, 64, 16)
    cj = c.reshape(2, 2, 16, 8, 16, 64)
    for g8 in range(8):
        bL1[:, :, :, g8, :, g8, :] = bj[:, :, :, g8].transpose(0, 1, 2, 4, 3)
        q, g2 = g8 // 2, g8 % 2
        bL2[:, :, :, g2, :, q, g8, :] = bj[:, :, :, g8]
        cL2[:, :, :, g2, :, q, g8, :] = cj[:, :, :, g8].transpose(0, 1, 2, 4, 3)
    g["bL1"] = bL1.reshape(2, 2, 16, 128, 512)
    g["bL2"] = bL2.reshape(2, 2, 16, 128, 512)
    g["cL2"] = cL2.reshape(2, 2, 16, 128, 512)
    d = np.asarray(inp["ssm_d"], f).reshape(2, 16, 128)
    dm = np.zeros((2, 16, 128, 128), f)
    idx = np.arange(128)
    dm[:, :, idx, idx] = d
    g["dmat"] = dm
    cst = np.zeros((128, 1024), f)
    cst[idx, idx] = 1.0
    jj, ss = np.meshgrid(idx, idx, indexing="ij")
    cst[:, 128:256] = (jj > ss)
    cst[:, 256:384] = (jj <= ss)
    cst[:, 384:896] = np.tile((jj < ss).astype(f), (1, 4))
    g["consts"] = cst
    for k in ("ssm_w_in", "ssm_w_glu", "w_k", "w_v", "w_q", "w_o", "mlp_w1", "mlp_w2", "ple_w", "ple_gate"):
        g[k] = np.ascontiguousarray(np.asarray(inp[k], f))
    return g


def run(inp, S, nseq_total, ncores, plan):
    shared = host_layouts(inp)
    x = np.asarray(inp["x"], np.float32)
    p = np.asarray(inp["p"], np.float32)
    nseq = nseq_total // ncores
    nc = build(S, nseq, plan)
    in_maps = []
    for c in range(ncores):
        m = dict(shared)
        m["x"] = np.ascontiguousarray(x[c * nseq:(c + 1) * nseq])
        m["pT"] = np.ascontiguousarray(p[:, c * nseq:(c + 1) * nseq].transpose(0, 1, 3, 2))
        in_maps.append(m)
    res = run_bass_kernel_spmd(nc, in_maps, core_ids=list(range(ncores)))
    return np.concatenate([r["out"] for r in res.results], axis=0)


def kernel(**inputs):
    return run(inputs, 2048, 16, 8, full_plan())
```
